# Optimizing a Trainium2 kernel written in Bass

```python
import jax, jax.numpy as jnp
from jax import lax
import numpy as np

D_MODEL = 1024
BATCH = 16
SEQ = 4096
DEPTH = 4

CHUNK = 128
A_WIDTH = D_MODEL
A_HEADS = 8
A_HEAD_DIM = A_WIDTH // A_HEADS
B_WIDTH = D_MODEL
CONV_WIDTH = 31
FFN_HIDDEN = 4 * D_MODEL
N_BRANCH = 2
N_MOD = 6
IN_COLS = 2 * A_WIDTH + 2 * B_WIDTH + N_BRANCH * D_MODEL
EPS = 1e-6

kernel_name = "hybrid_gmlp_conformer_gated_adaln"


def rmsnorm(x, g):
    xf = x.astype(jnp.float32)
    y = xf * lax.rsqrt(jnp.mean(xf * xf, axis=-1, keepdims=True) + EPS)
    return (y * g.astype(jnp.float32)).astype(x.dtype)


def layernorm(x, g, b):
    xf = x.astype(jnp.float32)
    mu = jnp.mean(xf, axis=-1, keepdims=True)
    xc = xf - mu
    var = jnp.mean(xc * xc, axis=-1, keepdims=True)
    y = xc * lax.rsqrt(var + EPS) * g.astype(jnp.float32) + b.astype(jnp.float32)
    return y.astype(x.dtype)


def chunked_spatial_gating(u, v, ln_g, ln_b, w_s, b_s):
    bsz, t, _ = v.shape
    v = layernorm(v, ln_g, ln_b)
    v = v.reshape(bsz, t // CHUNK, CHUNK, A_HEADS, A_HEAD_DIM)
    causal = jnp.tril(jnp.ones((CHUNK, CHUNK), dtype=bool))
    w = jnp.where(causal[None], w_s, 0).astype(v.dtype)
    s = jnp.einsum('hij,bnjhd->bnihd', w, v) + b_s.T.astype(v.dtype)[None, None, :, :, None]
    return u * s.reshape(bsz, t, A_WIDTH)


def conformer_conv(p, conv_w, conv_b, ln_g, ln_b):
    a, g = jnp.split(p, 2, axis=-1)
    z = a * jax.nn.sigmoid(g)
    z = lax.conv_general_dilated(
        z, conv_w.astype(z.dtype), window_strides=(1,),
        padding=[(CONV_WIDTH - 1, 0)],
        dimension_numbers=('NWC', 'WIO', 'NWC'),
        feature_group_count=B_WIDTH) + conv_b.astype(z.dtype)
    z = layernorm(z, ln_g, ln_b)
    return jax.nn.silu(z)


def setup_inputs(seed: int = 0) -> dict:
    key = jax.random.key(seed)
    ks = jax.random.split(key, 24)
    f32 = jnp.float32
    L, D = DEPTH, D_MODEL

    def nrm(k, shape, scale):
        return jax.random.normal(k, shape, f32) * scale

    return {
        "x": nrm(ks[0], (BATCH, SEQ, D), 1.0),
        "c": nrm(ks[1], (BATCH, D), 1.0),
        "w_ada": nrm(ks[2], (L, D, N_MOD * D), 0.5 * D ** -0.5),
        "b_ada": nrm(ks[3], (L, N_MOD * D), 0.02),
        "norm1_g": 1.0 + nrm(ks[4], (L, D), 0.05),
        "w_in": nrm(ks[5], (L, D, IN_COLS), D ** -0.5),
        "a_ln_g": 1.0 + nrm(ks[6], (L, A_WIDTH), 0.05),
        "a_ln_b": nrm(ks[7], (L, A_WIDTH), 0.02),
        "a_ws": nrm(ks[8], (L, A_HEADS, CHUNK, CHUNK), CHUNK ** -0.5),
        "a_bs": 1.0 + nrm(ks[9], (L, A_HEADS, CHUNK), 0.1),
        "w_pa": nrm(ks[10], (L, A_WIDTH, D), A_WIDTH ** -0.5),
        "b_conv_w": nrm(ks[11], (L, CONV_WIDTH, 1, B_WIDTH), CONV_WIDTH ** -0.5),
        "b_conv_b": nrm(ks[12], (L, B_WIDTH), 0.02),
        "b_ln_g": 1.0 + nrm(ks[13], (L, B_WIDTH), 0.05),
        "b_ln_b": nrm(ks[14], (L, B_WIDTH), 0.02),
        "w_pb": nrm(ks[15], (L, B_WIDTH, D), B_WIDTH ** -0.5),
        "w_out": nrm(ks[16], (L, D, D), D ** -0.5),
        "norm2_g": 1.0 + nrm(ks[17], (L, D), 0.05),
        "w_ff1": nrm(ks[18], (L, D, FFN_HIDDEN), D ** -0.5),
        "w_ff2": nrm(ks[19], (L, FFN_HIDDEN, D), FFN_HIDDEN ** -0.5),
        "final_g": 1.0 + nrm(ks[20], (D,), 0.05),
    }


def reference(x, c, w_ada, b_ada, norm1_g, w_in, a_ln_g, a_ln_b, a_ws, a_bs, w_pa,
              b_conv_w, b_conv_b, b_ln_g, b_ln_b, w_pb, w_out, norm2_g,
              w_ff1, w_ff2, final_g):
    split_at = [A_WIDTH, 2 * A_WIDTH, 2 * A_WIDTH + 2 * B_WIDTH,
                2 * A_WIDTH + 2 * B_WIDTH + D_MODEL]
    c_act = jax.nn.silu(c)
    for l in range(DEPTH):
        mod = (c_act @ w_ada[l] + b_ada[l])[:, None, :]
        sh1, sc1, gt1, sh2, sc2, gt2 = jnp.split(mod, N_MOD, axis=-1)

        h = rmsnorm(x, norm1_g[l]) * (1 + sc1) + sh1
        proj = h @ w_in[l]
        u, v, p_b, g_a, g_b = jnp.split(proj, split_at, axis=-1)
        y_a = chunked_spatial_gating(u, v, a_ln_g[l], a_ln_b[l], a_ws[l], a_bs[l]) @ w_pa[l]
        y_b = conformer_conv(p_b, b_conv_w[l], b_conv_b[l], b_ln_g[l], b_ln_b[l]) @ w_pb[l]
        merged = jax.nn.sigmoid(g_a) * y_a + jax.nn.sigmoid(g_b) * y_b
        x = x + gt1 * (merged @ w_out[l])

        h = rmsnorm(x, norm2_g[l]) * (1 + sc2) + sh2
        x = x + gt2 * (jnp.square(jax.nn.relu(h @ w_ff1[l])) @ w_ff2[l])

    return rmsnorm(x, final_g)
```

```python
import numpy as np
import concourse.bass as bass
import concourse.mybir as mybir
from concourse.bass_utils import run_bass_kernel_spmd

F32 = mybir.dt.float32
BF16 = mybir.dt.bfloat16
AF = mybir.ActivationFunctionType
ALU = mybir.AluOpType

L = 4
D = 1024
KC = 8
SEQ = 4096
NCORES = 8
NT = 1024
ST = 512
NSUB = NT // ST
CW = 31
HALO = CW - 1
EPS = 1e-6
NSLOT = 6
K_DB = 14
K_PE = 22
PIECES_PER_LAYER = 34
ADA_PIECES = 12

OFF = {}
_o = 0
for _name, _n in [("n1g", L * 8), ("n2g", L * 8), ("alg", L * 8), ("alb", L * 8), ("cvb", L * 8),
                  ("blg", L * 8), ("blb", L * 8), ("fg", 8), ("bada", L * 48), ("cvw", L * 8 * CW)]:
    OFF[_name] = _o
    _o += _n
NPARAM = _o


class Buf:
    __slots__ = ("name", "last_w", "reads")

    def __init__(self, name):
        self.name = name
        self.last_w = None
        self.reads = []


class Prog:
    ENGS = ("pe", "act", "dve", "pool", "sp")

    def __init__(self):
        self.q = {e: [] for e in self.ENGS}
        self.count = {}
        self.waited = {e: {} for e in self.ENGS}

    def _deps(self, eng, reads, writes):
        deps = {}

        def add(tok, kind):
            if tok is None:
                return
            key, val = tok
            if key == eng:
                if eng == "pe":
                    return
                if kind != "raw":
                    return
            if deps.get(key, 0) < val:
                deps[key] = val

        for b in reads:
            add(b.last_w, "raw")
        for b in writes:
            add(b.last_w, "waw")
            for t in b.reads:
                add(t, "war")
        out = []
        w = self.waited[eng]
        for key, val in deps.items():
            if w.get(key, 0) < val:
                w[key] = val
                out.append((key, val))
        return out

    def op(self, eng, fn, reads=(), writes=()):
        waits = self._deps(eng, reads, writes)
        self.count[eng] = self.count.get(eng, 0) + 1
        tok = (eng, self.count[eng])
        self.q[eng].append((waits, fn, (eng, 1)))
        for b in reads:
            b.reads.append(tok)
        for b in writes:
            b.last_w = tok
            b.reads = []
        return tok

    def dma(self, eng, semkey, fn, reads=(), writes=()):
        waits = self._deps(eng, reads, writes)
        self.count[semkey] = self.count.get(semkey, 0) + 16
        tok = (semkey, self.count[semkey])
        self.q[eng].append((waits, fn, (semkey, 16)))
        for b in reads:
            b.reads.append(tok)
        for b in writes:
            b.last_w = tok
            b.reads = []
        return tok

    def dma_group(self, eng, semkey, items):
        toks = []
        allw = []
        for fn, reads, writes in items:
            self.dma(eng, semkey, fn, reads=reads, writes=writes)
            allw += list(writes)
        final = (semkey, self.count[semkey])
        for b in allw:
            b.last_w = final
        return final

    def final_wait(self, eng, toks):
        self.q[eng].append(([t for t in toks], None, None))


def bc_mid(ap2d, reps):
    a = ap2d.ap
    return bass.AP(ap2d.tensor, ap2d.offset, [[a[0][0], a[0][1]], [0, reps], [a[-1][0], a[-1][1]]])


def bc_last(ap2d, reps):
    a = ap2d.ap
    return bass.AP(ap2d.tensor, ap2d.offset, [[a[0][0], a[0][1]], [a[-1][0], a[-1][1]], [0, reps]])


def build_nc(n_tiles=8, n_layers=L, tiles_per_seq=4):
    nc = bass.Bass("TRN2", target_bir_lowering=False)
    NTOK = n_tiles * NT
    xT = nc.dram_tensor("xT", [D, NTOK], F32, kind="ExternalInput").ap()
    outT = nc.dram_tensor("outT", [D, NTOK], F32, kind="ExternalOutput").ap()
    params_d = nc.dram_tensor("params", [128, NPARAM], F32, kind="ExternalInput").ap()
    cT_d = nc.dram_tensor("cT", [128, 16], F32, kind="ExternalInput").ap()
    wada_d = nc.dram_tensor("wada", [L * ADA_PIECES * 128, 4096], F32, kind="ExternalInput").ap()
    wst_d = nc.dram_tensor("wst", [L * PIECES_PER_LAYER * 128, 4096], F32, kind="ExternalInput").ap()
    wt_d = nc.dram_tensor("wsT", [128, L * 8 * 128], F32, kind="ExternalInput").ap()
    mask_d = nc.dram_tensor("mask", [128, 128], F32, kind="ExternalInput").ap()
    ident_d = nc.dram_tensor("ident", [128, 128], F32, kind="ExternalInput").ap()
    bs_d = nc.dram_tensor("bsrow", [1, L * 8 * 128], F32, kind="ExternalInput")

    P = Prog()
    n_seq = n_tiles // tiles_per_seq

    from contextlib import ExitStack
    with ExitStack() as es:
        def sb(name, shape, dt):
            return es.enter_context(nc.sbuf_tensor(name, shape, dt))

        def sem(name):
            return es.enter_context(nc.semaphore(name))

        X = sb("X", [128, KC, NT], F32)
        A = sb("A", [128, KC, NT], BF16)
        U = sb("U", [128, 24576], BF16)
        ring = [sb(f"ring{i}", [128, 4096], BF16) for i in range(NSLOT)]
        PRM = sb("PRM", [128, NPARAM], F32)
        MOD = sb("MOD", [128, L * 48 * 2], F32)
        GS = sb("GS", [128, L * 2 * 8 * 2], F32)
        WMT = sb("WMT", [128, L * 8 * 128], BF16)
        CB = sb("CB", [128, 8 * 128], F32)
        MASK = sb("MASK", [128, 128], F32)
        ONES_S = sb("ONES_S", [128, 128], BF16)
        ONES_1 = sb("ONES_1", [128, 128], BF16)
        EPSB = sb("EPSB", [128, 1], F32)
        CIN = sb("CIN", [128, 16], F32)
        CSG = sb("CSG", [128, 16], F32)
        CACT = sb("CACT", [128, 16], BF16)
        ZT = sb("ZT", [128, L * 8 * HALO], BF16)
        ZB = [sb(f"ZB{i}", [128, HALO + ST], BF16) for i in range(3)]
        SQ = [sb(f"SQ{i}", [128, ST], BF16) for i in range(2)]
        FT = [sb(f"FT{i}", [128, ST], F32) for i in range(5)]
        ST1 = [sb(f"ST1_{i}", [128, ST], F32) for i in range(1)]
        RSTD = [sb(f"RSTD{i}", [128, ST], F32) for i in range(2)]
        MU = [sb(f"MU{i}", [128, ST], F32) for i in range(1)]
        VH = [sb(f"VH{i}", [128, 4, D], BF16) for i in range(1)]
        BNS = [sb(f"BNS{i}", [128, 12], F32) for i in range(2)]
        MV = [sb(f"MV{i}", [128, 2], F32) for i in range(2)]
        VT = [sb(f"VT{i}", [128, 2], F32) for i in range(2)]
        DG = sb("DG", [128, (K_PE + K_DB) * 128], BF16)

        banks = [es.enter_context(nc.psum_tensor(f"bank{i}", [128, ST], F32)) for i in range(8)]

        UF = U[:, 0:16384].bitcast(F32)

        def ZCa(cc, s):
            o = (s * KC + cc) * ST
            return UF[:, o:o + ST]

        def Ta(h, s):
            o = (s * KC + h) * ST
            return U[:, o:o + ST]

        def MGa(oc, s):
            o = 8192 + (s * KC + oc) * ST
            return U[:, o:o + ST]
        Qv = U[:, 16384:24576].rearrange("p (k t) -> p k t", k=KC)
        Tv = U[:, 0:8192].rearrange("p (k t) -> p k t", k=KC)
        MGv = U[:, 8192:16384].rearrange("p (k t) -> p k t", k=KC)
        Rv = U[:, 0:16384].rearrange("p (k t) -> p k t", k=16)
        WTF = U[:, 0:8192].bitcast(F32)

        sems = {e: sem(f"s_{e}") for e in ("pe", "act", "dve", "pool")}
        for i in range(NSLOT):
            sems[f"ring{i}"] = sem(f"s_ring{i}")
        for k in ("ldx", "ldp", "st0", "st1", "st2", "st3", "st4", "bsd", "ldi"):
            sems[k] = sem(f"s_{k}")

        bX = [[Buf(f"X{k}_{s}") for s in range(NSUB)] for k in range(KC)]
        bA = [[Buf(f"A{k}_{s}") for s in range(NSUB)] for k in range(KC)]
        bUg = [Buf(f"U{g}") for g in range(48)]
        bU_ZC = [bUg[cc * 4:cc * 4 + 4] for cc in range(KC)]
        bU_ZCs = [[bUg[2 * (s * KC + cc):2 * (s * KC + cc) + 2] for s in range(NSUB)] for cc in range(KC)]
        bQ = [[[bUg[32 + k * 2 + s]] for s in range(NSUB)] for k in range(KC)]
        bT = [[[bUg[s * KC + k]] for s in range(NSUB)] for k in range(KC)]
        bMG = [[[bUg[16 + s * KC + k]] for s in range(NSUB)] for k in range(KC)]
        bR = [[[bUg[k * 2 + s]] for s in range(NSUB)] for k in range(16)]
        bring = [Buf(f"ring{i}") for i in range(NSLOT)]
        bbank = [Buf(f"bank{i}") for i in range(8)]
        bPRM = Buf("PRM"); bMOD = Buf("MOD"); bGS = Buf("GS"); bWMT = Buf("WMT"); bCB = Buf("CB")
        bMASK = Buf("MASK"); bCONST = Buf("CONST"); bCIN = Buf("CIN"); bCSG = Buf("CSG"); bCACT = Buf("CACT")
        bZT = [[Buf(f"ZT{l}_{k}") for k in range(KC)] for l in range(L)]
        bZB = [Buf(f"ZB{i}") for i in range(3)]
        bSQ = [Buf(f"SQ{i}") for i in range(2)]
        bFT = [Buf(f"FT{i}") for i in range(5)]
        bST1 = [Buf(f"ST1{i}") for i in range(1)]
        bRSTD = [Buf(f"RSTD{i}") for i in range(2)]
        bMU = [Buf(f"MU{i}") for i in range(1)]
        bVH = [[Buf(f"VH{i}_{c}") for c in range(4)] for i in range(1)]
        bBNS = [Buf(f"BNS{i}") for i in range(2)]
        bMV = [Buf(f"MV{i}") for i in range(2)]
        bVT = [Buf(f"VT{i}") for i in range(2)]
        bDG = [Buf(f"DG{k}") for k in range(K_PE + K_DB)]
        bWTF = bUg[0:16]

        rot = {}

        def nxt(name, n):
            i = rot.get(name, 0)
            rot[name] = (i + 1) % n
            return i

        piece_src = []
        for l in range(L):
            for j in range(ADA_PIECES):
                r0 = (l * ADA_PIECES + j) * 128
                piece_src.append(wada_d[r0:r0 + 128, :])
        n_ada = len(piece_src)
        for t in range(n_tiles):
            for l in range(n_layers):
                for j in range(PIECES_PER_LAYER):
                    r0 = (l * PIECES_PER_LAYER + j) * 128
                    piece_src.append(wst_d[r0:r0 + 128, :])
        issued = [0]

        def issue_piece(n):
            assert n == issued[0]
            if n >= len(piece_src):
                return
            slot = n % NSLOT
            src = piece_src[n]
            P.dma("pool", f"ring{slot}",
                  lambda e, slot=slot, src=src: e.dma_start(out=ring[slot][:], in_=src),
                  writes=[bring[slot]])
            issued[0] += 1

        def release_piece(n):
            issue_piece_target = n + NSLOT
            while issued[0] <= issue_piece_target and issued[0] < len(piece_src):
                issue_piece(issued[0])

        for n in range(NSLOT):
            issue_piece(n)

        def slot_of(n):
            assert n < issued[0], (n, issued[0])
            return ring[n % NSLOT], bring[n % NSLOT]

        def mm_group(bank_i, pairs, reads, out_ap=None, writes=None):
            o = out_ap if out_ap is not None else banks[bank_i][:]
            n = len(pairs)

            def fn(e, o=o, pairs=pairs, n=n):
                ins = None
                for i, (lt, rh) in enumerate(pairs):
                    ins = e.matmul(o, lt, rh, start=(i == 0), stop=(i == n - 1))
                return ins
            return P.op("pe", fn, reads=reads, writes=writes if writes is not None else [bbank[bank_i]])

        def prm(name, idx):
            o = OFF[name] + idx
            return PRM[:, o:o + 1]

        def gs_ap(l, which, kc, s):
            o = ((l * 2 + which) * 8 + kc) * 2 + s
            return GS[:, o:o + 1]

        def mod_ap(l, m, kc, s):
            o = (l * 48 + m * 8 + kc) * 2 + s
            return MOD[:, o:o + 1]

        P.dma_group("sp", "ldp", [
            (lambda e: e.dma_start(out=PRM[:], in_=params_d[:, :]), [], [bPRM]),
            (lambda e: e.dma_start(out=CIN[:], in_=cT_d[:, :]), [], [bCIN]),
            (lambda e: e.dma_start(out=MASK[:], in_=mask_d[:, :]), [], [bMASK]),
            (lambda e: e.dma_start(out=WTF, in_=wt_d[:, :]), [], bWTF),
        ])
        P.op("dve", lambda e: e.memset(ONES_S[:], 1.0 / 1024.0), writes=[bCONST])
        P.op("dve", lambda e: e.memset(ONES_1[:], 1.0), writes=[bCONST])
        P.op("dve", lambda e: e.memset(EPSB[:], EPS), writes=[bCONST])
        P.op("act", lambda e: e.activation(out=CSG[:], in_=CIN[:], func=AF.Sigmoid), reads=[bCIN], writes=[bCSG])
        P.op("dve", lambda e: e.tensor_tensor(out=CACT[:], in0=CIN[:], in1=CSG[:], op=ALU.mult),
             reads=[bCIN, bCSG], writes=[bCACT])
        pn = 0
        for l in range(L):
            for j in range(ADA_PIECES):
                slot, bslot = slot_of(pn)
                bk = nxt("bank", 6)
                first = True
                for ocl in range(4):
                    pairs = [(slot[:, kc * 512 + ocl * 128: kc * 512 + (ocl + 1) * 128], CACT[:, kc * 2:(kc + 1) * 2])
                             for kc in range(KC)]
                    mm_group(bk, pairs, reads=[bslot, bCACT], out_ap=banks[bk][:, ocl * 2:(ocl + 1) * 2],
                             writes=[bbank[bk]] if first else [])
                    if not first:
                        bbank[bk].last_w = ("pe", P.count["pe"])
                    first = False
                o = (l * 48 + 4 * j) * 2
                bo = OFF["bada"] + l * 48 + 4 * j
                P.op("dve", lambda e, bk=bk, o=o, bo=bo: e.tensor_tensor(
                    out=MOD[:, o:o + 8].rearrange("p (c s) -> p c s", s=2),
                    in0=banks[bk][:, 0:8].rearrange("p (c s) -> p c s", s=2),
                    in1=bc_last(PRM[:, bo:bo + 4], 2), op=ALU.add),
                    reads=[bbank[bk], bPRM], writes=[bMOD])
                release_piece(pn)
                pn += 1
        for l in range(L):
            for which, m, gname in ((0, 1, "n1g"), (1, 4, "n2g")):
                o = (l * 2 + which) * 16
                mo = (l * 48 + m * 8) * 2
                go = OFF[gname] + l * 8
                P.op("dve", lambda e, o=o, mo=mo: e.tensor_scalar(
                    out=GS[:, o:o + 16], in0=MOD[:, mo:mo + 16], scalar1=1.0, scalar2=None, op0=ALU.add),
                    reads=[bMOD], writes=[bGS])
                P.op("dve", lambda e, o=o, go=go: e.tensor_tensor(
                    out=GS[:, o:o + 16].rearrange("p (c s) -> p c s", s=2),
                    in0=GS[:, o:o + 16].rearrange("p (c s) -> p c s", s=2),
                    in1=bc_last(PRM[:, go:go + 8], 2), op=ALU.mult),
                    reads=[bGS, bPRM], writes=[bGS])
        P.op("dve", lambda e: e.tensor_tensor(
            out=WMT[:].rearrange("p (g i) -> p g i", i=128),
            in0=WTF.rearrange("p (g i) -> p g i", i=128),
            in1=bc_mid(MASK[:], L * 8), op=ALU.mult),
            reads=bWTF + [bMASK], writes=[bWMT])
        P.dma("sp", "ldi", lambda e: e.dma_start(out=MASK[:], in_=ident_d[:, :]), writes=[bMASK])
        def gate_prep(l):
            P.dma("sp", "bsd", lambda e: e.dma_start(
                out=CB[:], in_=bass.AP(bs_d, l * 1024, [[0, 128], [1, 1024]])), writes=[bCB])
            for hf in range(2):
                bk = nxt("bank", 6)
                c0 = l * 1024 + hf * 512
                mm_group(bk, [(ONES_1[:], WMT[:, c0:c0 + 512])], reads=[bWMT, bCONST])
                for hh in range(4):
                    h = hf * 4 + hh
                    cc0 = h * 128
                    bidx = OFF["alb"] + l * 8 + h
                    P.op("dve", lambda e, bk=bk, hh=hh, cc0=cc0, bidx=bidx: e.scalar_tensor_tensor(
                        out=CB[:, cc0:cc0 + 128], in0=banks[bk][:, hh * 128:(hh + 1) * 128],
                        scalar=PRM[:, bidx:bidx + 1], in1=CB[:, cc0:cc0 + 128], op0=ALU.mult, op1=ALU.add),
                        reads=[bbank[bk], bPRM, bCB], writes=[bCB])

        def rms_stats(src_bufs_fn, src_ap_fn, s):
            bk = nxt("bank", 6)
            for kc in range(KC):
                qi = nxt("SQ", 2)
                P.op("act", lambda e, kc=kc, qi=qi: e.activation(out=SQ[qi][:], in_=src_ap_fn(kc, s), func=AF.Square),
                     reads=[src_bufs_fn(kc, s)], writes=[bSQ[qi]])

                def fn(e, kc=kc, qi=qi, bk=bk):
                    return e.matmul(banks[bk][:], ONES_S[:], SQ[qi][:], start=(kc == 0), stop=(kc == KC - 1))
                P.op("pe", fn, reads=[bSQ[qi], bCONST], writes=[bbank[bk]] if kc == 0 else [])
            bbank[bk].last_w = ("pe", P.count["pe"])
            si = nxt("ST1", 1)
            P.op("act", lambda e, bk=bk, si=si: e.activation(
                out=ST1[si][:], in_=banks[bk][:], func=AF.Sqrt, bias=EPSB[:, 0:1]),
                reads=[bbank[bk], bCONST], writes=[bST1[si]])
            ri = nxt("RSTD", 2)
            P.op("dve", lambda e, si=si, ri=ri: e.reciprocal(out=RSTD[ri][:], in_=ST1[si][:]),
                 reads=[bST1[si]], writes=[bRSTD[ri]])
            return ri

        def norm_gen(l, which, sq, s):
            shm = 0 if which == 0 else 3
            ri = rms_stats(lambda kc, s: bX[kc][s], lambda kc, s: X[:, kc, s * ST:(s + 1) * ST], s)
            yield
            for kc in range(KC):
                fi = nxt("FT", 5)
                P.op("dve", lambda e, kc=kc, s=s, fi=fi, ri=ri: e.tensor_tensor(
                    out=FT[fi][:], in0=X[:, kc, s * ST:(s + 1) * ST], in1=RSTD[ri][:], op=ALU.mult),
                    reads=[bX[kc][s], bRSTD[ri]], writes=[bFT[fi]])
                P.op("act", lambda e, kc=kc, s=s, fi=fi: e.activation(
                    out=A[:, kc, s * ST:(s + 1) * ST], in_=FT[fi][:], func=AF.Identity,
                    scale=gs_ap(l, which, kc, sq), bias=mod_ap(l, shm, kc, sq)),
                    reads=[bFT[fi], bGS, bMOD], writes=[bA[kc][s]])
                if kc % 2 == 1:
                    yield

        def run(g):
            for _ in g:
                pass

        def interleave(*gens):
            gens = list(gens)
            while gens:
                for g in list(gens):
                    try:
                        next(g)
                    except StopIteration:
                        gens.remove(g)

        def A_all(s):
            return [bA[kc][s] for kc in range(KC)]

        def conv_pe_gen(l, pn0, s, zero_halo):
            zis = {}
            tsl = slice(s * ST, (s + 1) * ST)

            def ag(cc):
                half, cl = divmod(cc, 4)
                sa, bsa = slot_of(pn0 + 2 * half)
                sg, bsg = slot_of(pn0 + 2 * half + 1)
                zi = nxt("ZB", 3)
                zis[cc] = zi
                zo = (l * 8 + cc) * HALO
                if zero_halo:
                    P.op("act", lambda e, zi=zi: e.memzero(ZB[zi][:, 0:HALO]), writes=[bZB[zi]])
                else:
                    P.op("act", lambda e, zi=zi, zo=zo: e.copy(out=ZB[zi][:, 0:HALO], in_=ZT[:, zo:zo + HALO]),
                         reads=[bZT[l][cc]], writes=[bZB[zi]])
                ba = nxt("bank", 6)
                mm_group(ba, [(sa[:, kc * 512 + cl * 128: kc * 512 + (cl + 1) * 128], A[:, kc, tsl])
                              for kc in range(KC)], reads=[bsa] + A_all(s))
                bg = nxt("bank", 6)
                mm_group(bg, [(sg[:, kc * 512 + cl * 128: kc * 512 + (cl + 1) * 128], A[:, kc, tsl])
                              for kc in range(KC)], reads=[bsg] + A_all(s))
                fi = nxt("FT", 5)
                P.op("act", lambda e, bg=bg, fi=fi: e.activation(out=FT[fi][:], in_=banks[bg][:], func=AF.Sigmoid),
                     reads=[bbank[bg]], writes=[bFT[fi]])
                P.op("dve", lambda e, ba=ba, fi=fi, zi=zi: e.tensor_tensor(
                    out=ZB[zi][:, HALO:HALO + ST], in0=banks[ba][:], in1=FT[fi][:], op=ALU.mult),
                    reads=[bbank[ba], bFT[fi]], writes=[bZB[zi]])

            dgs = {}

            def dgi(par, k):
                return par * K_DB + k if k < K_DB else 2 * K_DB + (k - K_DB)

            def diag(cc, lo, hi):
                if lo == 0:
                    dgs[cc] = nxt("DGP", 2)
                par = dgs[cc]
                wo = OFF["cvw"] + (l * 8 + cc) * CW
                for k in range(lo, hi):
                    d0 = dgi(par, k) * 128
                    P.op("act", lambda e, k=k, wo=wo, d0=d0: e.activation(
                        out=DG[:, d0:d0 + 128], in_=MASK[:], func=AF.Identity, scale=PRM[:, wo + k:wo + k + 1]),
                        reads=[bMASK, bPRM], writes=[bDG[dgi(par, k)]])
                if lo == 0:
                    zi = zis[cc]
                    zo = (l * 8 + cc) * HALO
                    P.op("act", lambda e, zi=zi, zo=zo: e.copy(out=ZT[:, zo:zo + HALO], in_=ZB[zi][:, ST:ST + HALO]),
                         reads=[bZB[zi]], writes=[bZT[l][cc]])

            def taps(cc):
                zi = zis[cc]
                par = dgs[cc]
                wo = OFF["cvw"] + (l * 8 + cc) * CW
                bo = OFF["cvb"] + l * 8 + cc
                bk = nxt("bank", 6)
                for k in range(K_PE):
                    d0 = dgi(par, k) * 128
                    P.op("pe", lambda e, k=k, zi=zi, bk=bk, d0=d0: e.matmul(
                        banks[bk][:], DG[:, d0:d0 + 128], ZB[zi][:, k:k + ST],
                        start=(k == 0), stop=(k == K_PE - 1)),
                        reads=[bDG[dgi(par, k)], bZB[zi]], writes=[bbank[bk]] if k == 0 else [])
                bbank[bk].last_w = ("pe", P.count["pe"])
                P.op("dve", lambda e, cc=cc, bk=bk, bo=bo: e.tensor_scalar(
                    out=ZCa(cc, s), in0=banks[bk][:], scalar1=PRM[:, bo:bo + 1], scalar2=None,
                    op0=ALU.add), reads=[bbank[bk], bPRM], writes=bU_ZCs[cc][s])
                for k in range(K_PE, CW):
                    P.op("dve", lambda e, cc=cc, zi=zi, wo=wo, k=k: e.scalar_tensor_tensor(
                        out=ZCa(cc, s), in0=ZB[zi][:, k:k + ST], scalar=PRM[:, wo + k:wo + k + 1],
                        in1=ZCa(cc, s), op0=ALU.mult, op1=ALU.add),
                        reads=[bZB[zi], bPRM] + bU_ZCs[cc][s], writes=bU_ZCs[cc][s])

            def stats(cc):
                q1 = nxt("SQ", 2)
                P.op("act", lambda e, cc=cc, q1=q1: e.copy(out=SQ[q1][:], in_=ZCa(cc, s)),
                     reads=bU_ZCs[cc][s], writes=[bSQ[q1]])
                P.op("pe", lambda e, cc=cc, q1=q1: e.matmul(
                    banks[6][:], ONES_S[:], SQ[q1][:], start=(cc == 0), stop=(cc == KC - 1)),
                    reads=[bSQ[q1], bCONST], writes=[bbank[6]] if cc == 0 else [])
                q2 = nxt("SQ", 2)
                P.op("act", lambda e, cc=cc, q2=q2: e.activation(out=SQ[q2][:], in_=ZCa(cc, s), func=AF.Square),
                     reads=bU_ZCs[cc][s], writes=[bSQ[q2]])
                P.op("pe", lambda e, cc=cc, q2=q2: e.matmul(
                    banks[7][:], ONES_S[:], SQ[q2][:], start=(cc == 0), stop=(cc == KC - 1)),
                    reads=[bSQ[q2], bCONST], writes=[bbank[7]] if cc == 0 else [])
                bbank[6].last_w = ("pe", P.count["pe"])
                bbank[7].last_w = ("pe", P.count["pe"])

            ag(0)
            diag(0, 0, K_DB)
            diag(0, K_DB, K_PE)
            for cc in range(KC):
                if cc + 1 < KC:
                    ag(cc + 1)
                    diag(cc + 1, 0, K_DB)
                taps(cc)
                if cc + 1 < KC:
                    diag(cc + 1, K_DB, K_PE)
                if cc >= 1:
                    stats(cc - 1)
                yield
            stats(KC - 1)

        def ln_gen(l, s):
            bm, bx = 6, 7
            mi = nxt("MU", 1)
            P.op("dve", lambda e, bm=bm, mi=mi: e.tensor_copy(out=MU[mi][:], in_=banks[bm][:]),
                 reads=[bbank[bm]], writes=[bMU[mi]])
            si = nxt("ST1", 1)
            P.op("dve", lambda e, mi=mi, si=si: e.tensor_tensor(out=ST1[si][:], in0=MU[mi][:], in1=MU[mi][:], op=ALU.mult),
                 reads=[bMU[mi]], writes=[bST1[si]])
            P.op("dve", lambda e, bx=bx, si=si: e.scalar_tensor_tensor(
                out=ST1[si][:], in0=banks[bx][:], scalar=EPS, in1=ST1[si][:], op0=ALU.add, op1=ALU.subtract),
                reads=[bbank[bx], bST1[si]], writes=[bST1[si]])
            P.op("act", lambda e, si=si: e.activation(out=ST1[si][:], in_=ST1[si][:], func=AF.Sqrt),
                 reads=[bST1[si]], writes=[bST1[si]])
            ri = nxt("RSTD", 2)
            P.op("dve", lambda e, si=si, ri=ri: e.reciprocal(out=RSTD[ri][:], in_=ST1[si][:]),
                 reads=[bST1[si]], writes=[bRSTD[ri]])
            yield
            for cc in range(KC):
                f1 = nxt("FT", 5)
                P.op("dve", lambda e, cc=cc, f1=f1, mi=mi: e.tensor_tensor(
                    out=FT[f1][:], in0=ZCa(cc, s), in1=MU[mi][:], op=ALU.subtract),
                    reads=bU_ZCs[cc][s] + [bMU[mi]], writes=[bFT[f1]])
                f2 = nxt("FT", 5)
                P.op("dve", lambda e, f1=f1, f2=f2, ri=ri: e.tensor_tensor(
                    out=FT[f2][:], in0=FT[f1][:], in1=RSTD[ri][:], op=ALU.mult),
                    reads=[bFT[f1], bRSTD[ri]], writes=[bFT[f2]])
                gi = OFF["blg"] + l * 8 + cc
                bi = OFF["blb"] + l * 8 + cc
                f3 = nxt("FT", 5)
                P.op("act", lambda e, f2=f2, f3=f3, gi=gi, bi=bi: e.activation(
                    out=FT[f3][:], in_=FT[f2][:], func=AF.Sigmoid, scale=PRM[:, gi:gi + 1], bias=PRM[:, bi:bi + 1]),
                    reads=[bFT[f2], bPRM], writes=[bFT[f3]])
                P.op("act", lambda e, f2=f2, f1=f1, gi=gi, bi=bi: e.activation(
                    out=FT[f1][:], in_=FT[f2][:], func=AF.Identity, scale=PRM[:, gi:gi + 1], bias=PRM[:, bi:bi + 1]),
                    reads=[bFT[f2], bPRM], writes=[bFT[f1]])
                P.op("dve", lambda e, cc=cc, f1=f1, f3=f3: e.tensor_tensor(
                    out=Qv[:, cc, s * ST:(s + 1) * ST], in0=FT[f1][:], in1=FT[f3][:], op=ALU.mult),
                    reads=[bFT[f1], bFT[f3]], writes=bQ[cc][s])
                yield

        def gate_gen(l, pn0, s):
            sv = [slot_of(pn0), slot_of(pn0 + 1)]
            su = [slot_of(pn0 + 2), slot_of(pn0 + 3)]
            if True:
                for c in range(4):
                    tok0 = s * ST + c * 128
                    pv = []
                    for hv in range(2):
                        bk = nxt("bank", 6)
                        mm_group(bk, [(A[:, kc, tok0:tok0 + 128], sv[hv][0][:, kc * 512:(kc + 1) * 512]) for kc in range(KC)],
                                 reads=[sv[hv][1]] + A_all(s))
                        pv.append(bk)
                    bi = nxt("BNS", 2)
                    P.op("dve", lambda e, bi=bi, b0=pv[0]: e.bn_stats(out=BNS[bi][:, 0:6], in_=banks[b0][:]),
                         reads=[bbank[pv[0]]], writes=[bBNS[bi]])
                    P.op("dve", lambda e, bi=bi, b1=pv[1]: e.bn_stats(out=BNS[bi][:, 6:12], in_=banks[b1][:]),
                         reads=[bbank[pv[1]], bBNS[bi]], writes=[bBNS[bi]])
                    P.op("dve", lambda e, bi=bi: e.bn_aggr(out=MV[bi][:], in_=BNS[bi][:]),
                         reads=[bBNS[bi]], writes=[bMV[bi]])
                    P.op("act", lambda e, bi=bi: e.activation(
                        out=VT[bi][:, 0:1], in_=MV[bi][:, 1:2], func=AF.Sqrt, bias=EPSB[:, 0:1]),
                        reads=[bMV[bi], bCONST], writes=[bVT[bi]])
                    P.op("dve", lambda e, bi=bi: e.reciprocal(out=VT[bi][:, 1:2], in_=VT[bi][:, 0:1]),
                         reads=[bVT[bi]], writes=[bVT[bi]])
                    for hv in range(2):
                        P.op("dve", lambda e, bi=bi, c=c, hv=hv, bk=pv[hv]: e.tensor_scalar(
                            out=VH[0][:, c, hv * 512:(hv + 1) * 512], in0=banks[bk][:],
                            scalar1=MV[bi][:, 0:1], scalar2=VT[bi][:, 1:2], op0=ALU.subtract, op1=ALU.mult),
                            reads=[bbank[pv[hv]], bMV[bi], bVT[bi]], writes=[bVH[0][c]])
                    yield
                for h in range(KC):
                    usl, busl = su[h // 4]
                    hl = h % 4
                    bu = nxt("bank", 6)
                    mm_group(bu, [(usl[:, kc * 512 + hl * 128: kc * 512 + (hl + 1) * 128], A[:, kc, s * ST:(s + 1) * ST])
                                  for kc in range(KC)], reads=[busl] + A_all(s))
                    bs0 = nxt("bank", 6)
                    wo = (l * 8 + h) * 128
                    first = True
                    for c in range(4):
                        mm_group(bs0, [(VH[0][:, c, h * 128:(h + 1) * 128], WMT[:, wo:wo + 128])],
                                 reads=[bVH[0][c], bWMT], out_ap=banks[bs0][:, c * 128:(c + 1) * 128],
                                 writes=[bbank[bs0]] if first else [])
                        first = False
                    bbank[bs0].last_w = ("pe", P.count["pe"])
                    fi = nxt("FT", 5)
                    gi = OFF["alg"] + l * 8 + h
                    P.op("dve", lambda e, bs0=bs0, fi=fi, gi=gi, h=h: e.scalar_tensor_tensor(
                        out=FT[fi][:].rearrange("p (c i) -> p c i", c=4),
                        in0=banks[bs0][:].rearrange("p (c i) -> p c i", c=4),
                        scalar=PRM[:, gi:gi + 1], in1=bc_mid(CB[:, h * 128:(h + 1) * 128], 4), op0=ALU.mult, op1=ALU.add),
                        reads=[bbank[bs0], bPRM, bCB], writes=[bFT[fi]])
                    P.op("dve", lambda e, bu=bu, fi=fi, h=h, s=s: e.tensor_tensor(
                        out=Ta(h, s), in0=banks[bu][:], in1=FT[fi][:], op=ALU.mult),
                        reads=[bbank[bu], bFT[fi]], writes=bT[h][s])
                    yield

        def merge_stage(l, pn0):
            for hf in range(2):
                spa, bpa = slot_of(pn0 + 4 * hf)
                spb, bpb = slot_of(pn0 + 4 * hf + 1)
                sga, bga = slot_of(pn0 + 4 * hf + 2)
                sgb, bgb = slot_of(pn0 + 4 * hf + 3)
                for s in range(NSUB):
                    for ocl in range(4):
                        oc = hf * 4 + ocl

                        def wcol(sl, kc, ocl=ocl):
                            return sl[:, kc * 512 + ocl * 128: kc * 512 + (ocl + 1) * 128]
                        tsl = slice(s * ST, (s + 1) * ST)
                        bga_k = nxt("bank", 6)
                        mm_group(bga_k, [(wcol(sga, kc), A[:, kc, tsl]) for kc in range(KC)], reads=[bga] + A_all(s))
                        bgb_k = nxt("bank", 6)
                        mm_group(bgb_k, [(wcol(sgb, kc), A[:, kc, tsl]) for kc in range(KC)], reads=[bgb] + A_all(s))
                        bya = nxt("bank", 6)
                        mm_group(bya, [(wcol(spa, kc), Ta(kc, s)) for kc in range(KC)],
                                 reads=[bpa] + [bT[kc][s][0] for kc in range(KC)])
                        byb = nxt("bank", 6)
                        mm_group(byb, [(wcol(spb, kc), Qv[:, kc, tsl]) for kc in range(KC)],
                                 reads=[bpb] + [bQ[kc][s][0] for kc in range(KC)])
                        f1 = nxt("FT", 5)
                        P.op("act", lambda e, b=bga_k, f1=f1: e.activation(out=FT[f1][:], in_=banks[b][:], func=AF.Sigmoid),
                             reads=[bbank[bga_k]], writes=[bFT[f1]])
                        f2 = nxt("FT", 5)
                        P.op("act", lambda e, b=bgb_k, f2=f2: e.activation(out=FT[f2][:], in_=banks[b][:], func=AF.Sigmoid),
                             reads=[bbank[bgb_k]], writes=[bFT[f2]])
                        P.op("dve", lambda e, b=bya, f1=f1: e.tensor_tensor(out=FT[f1][:], in0=banks[b][:], in1=FT[f1][:], op=ALU.mult),
                             reads=[bbank[bya], bFT[f1]], writes=[bFT[f1]])
                        P.op("dve", lambda e, b=byb, f2=f2: e.tensor_tensor(out=FT[f2][:], in0=banks[b][:], in1=FT[f2][:], op=ALU.mult),
                             reads=[bbank[byb], bFT[f2]], writes=[bFT[f2]])
                        P.op("dve", lambda e, f1=f1, f2=f2, oc=oc, s=s: e.tensor_tensor(
                            out=MGa(oc, s), in0=FT[f1][:], in1=FT[f2][:], op=ALU.add),
                            reads=[bFT[f1], bFT[f2]], writes=bMG[oc][s])
                for i in range(4):
                    release_piece(pn0 + 4 * hf + i)

        def out_gen(l, pn0, sq, s):
            tsl = slice(s * ST, (s + 1) * ST)
            for hf in range(2):
                so, bso = slot_of(pn0 + hf)
                for ocl in range(4):
                    oc = hf * 4 + ocl
                    bk = nxt("bank", 6)
                    mm_group(bk, [(so[:, kc * 512 + ocl * 128: kc * 512 + (ocl + 1) * 128], MGa(kc, s)) for kc in range(KC)],
                             reads=[bso] + [bMG[kc][s][0] for kc in range(KC)])
                    P.op("dve", lambda e, bk=bk, oc=oc: e.scalar_tensor_tensor(
                        out=X[:, oc, tsl], in0=banks[bk][:], scalar=mod_ap(l, 2, oc, sq), in1=X[:, oc, tsl],
                        op0=ALU.mult, op1=ALU.add),
                        reads=[bbank[bk], bMOD, bX[oc][s]], writes=[bX[oc][s]])
                    if ocl % 2 == 1:
                        yield

        def ff1_gen(p1, s_list):
            for s in s_list:
                tsl = slice(s * ST, (s + 1) * ST)
                for j in range(4):
                    sf, bsf = slot_of(p1 + j)
                    for hcl in range(4):
                        hc = j * 4 + hcl
                        bk = nxt("bank", 6)
                        mm_group(bk, [(sf[:, kc * 512 + hcl * 128: kc * 512 + (hcl + 1) * 128], A[:, kc, tsl]) for kc in range(KC)],
                                 reads=[bsf] + A_all(s))
                        fi = nxt("FT", 5)
                        P.op("act", lambda e, bk=bk, fi=fi: e.activation(out=FT[fi][:], in_=banks[bk][:], func=AF.Relu),
                             reads=[bbank[bk]], writes=[bFT[fi]])
                        P.op("act", lambda e, fi=fi, hc=hc, tsl=tsl: e.activation(
                            out=Rv[:, hc, tsl], in_=FT[fi][:], func=AF.Square),
                            reads=[bFT[fi]], writes=bR[hc][s])
                        if hcl % 2 == 1:
                            yield

        def ff2_gen(l, p2, sq, s_list):
            for s in s_list:
                tsl = slice(s * ST, (s + 1) * ST)
                for q in range(4):
                    s2, bs2 = slot_of(p2 + q)
                    for ocl in range(2):
                        oc = q * 2 + ocl
                        bk = nxt("bank", 6)
                        mm_group(bk, [(s2[:, kc * 256 + ocl * 128: kc * 256 + (ocl + 1) * 128], Rv[:, kc, tsl]) for kc in range(16)],
                                 reads=[bs2] + [bR[kc][s][0] for kc in range(16)])
                        P.op("dve", lambda e, bk=bk, oc=oc, tsl=tsl: e.scalar_tensor_tensor(
                            out=X[:, oc, tsl], in0=banks[bk][:], scalar=mod_ap(l, 5, oc, sq), in1=X[:, oc, tsl],
                            op0=ALU.mult, op1=ALU.add),
                            reads=[bbank[bk], bMOD, bX[oc][s]], writes=[bX[oc][s]])
                        yield

        def final_stage(t):
            for s in range(NSUB):
                ri = rms_stats(lambda kc, s: bX[kc][s], lambda kc, s: X[:, kc, s * ST:(s + 1) * ST], s)
                for kc in range(KC):
                    fi = nxt("FT", 5)
                    P.op("dve", lambda e, kc=kc, s=s, fi=fi, ri=ri: e.tensor_tensor(
                        out=FT[fi][:], in0=X[:, kc, s * ST:(s + 1) * ST], in1=RSTD[ri][:], op=ALU.mult),
                        reads=[bX[kc][s], bRSTD[ri]], writes=[bFT[fi]])
                    oi = nxt("FT", 5)
                    fo = OFF["fg"] + kc
                    P.op("act", lambda e, fi=fi, oi=oi, fo=fo: e.activation(
                        out=FT[oi][:], in_=FT[fi][:], func=AF.Identity, scale=PRM[:, fo:fo + 1]),
                        reads=[bFT[fi], bPRM], writes=[bFT[oi]])
                    c0 = t * NT + s * ST
                    P.dma("sp", f"st{oi}", lambda e, oi=oi, kc=kc, c0=c0: e.dma_start(
                        out=outT[kc * 128:(kc + 1) * 128, c0:c0 + ST], in_=FT[oi][:]),
                        reads=[bFT[oi]])

        pn = n_ada
        for t in range(n_tiles):
            sq = t // tiles_per_seq
            first_in_seq = (t % tiles_per_seq == 0)
            P.dma_group("sp", "ldx", [
                (lambda e, kc=kc, t=t: e.dma_start(out=X[:, kc, :], in_=xT[kc * 128:(kc + 1) * 128, t * NT:(t + 1) * NT]),
                 [], [bX[kc][0], bX[kc][1]]) for kc in range(KC)])
            norm0_done = False
            for l in range(n_layers):
                gate_prep(l)
                if not norm0_done:
                    run(norm_gen(l, 0, sq, 0))
                interleave(norm_gen(l, 0, sq, 1), conv_pe_gen(l, pn, 0, first_in_seq))
                g0 = ln_gen(l, 0)
                next(g0)
                interleave(conv_pe_gen(l, pn, 1, False), g0)
                for i in range(4):
                    release_piece(pn + i)
                run(gate_gen(l, pn + 4, 0))
                interleave(ln_gen(l, 1), gate_gen(l, pn + 4, 1))
                for i in range(4):
                    release_piece(pn + 4 + i)
                merge_stage(l, pn + 8)
                run(out_gen(l, pn + 16, sq, 0))
                interleave(out_gen(l, pn + 16, sq, 1), norm_gen(l, 1, sq, 0))
                release_piece(pn + 16)
                release_piece(pn + 17)
                interleave(norm_gen(l, 1, sq, 1), ff1_gen(pn + 18, [0]))
                run(ff1_gen(pn + 18, [1]))
                for i in range(4):
                    release_piece(pn + 18 + i)
                run(ff2_gen(l, pn + 22, sq, [0, 1]))
                for i in range(4):
                    release_piece(pn + 22 + i)
                run(ff1_gen(pn + 26, [0, 1]))
                for i in range(4):
                    release_piece(pn + 26 + i)
                run(ff2_gen(l, pn + 30, sq, [0]))
                if l + 1 < n_layers:
                    interleave(ff2_gen(l, pn + 30, sq, [1]), norm_gen(l + 1, 0, sq, 0))
                    norm0_done = True
                else:
                    run(ff2_gen(l, pn + 30, sq, [1]))
                    norm0_done = False
                for i in range(4):
                    release_piece(pn + 30 + i)
                pn += PIECES_PER_LAYER
            final_stage(t)

        P.final_wait("sp", [(f"st{i}", P.count.get(f"st{i}", 0)) for i in range(5) if P.count.get(f"st{i}", 0)])

        with nc.Block() as block:
            def replay(engname):
                def run(e):
                    for waits, fn, sig in P.q[engname]:
                        for key, val in waits:
                            e.wait_ge(sems[key], val)
                        if fn is None:
                            continue
                        ins = fn(e)
                        ins.then_inc(sems[sig[0]], sig[1])
                return run

            block.tensor(replay("pe"))
            block.scalar(replay("act"))
            block.vector(replay("dve"))
            block.gpsimd(replay("pool"))
            block.sync(replay("sp"))
    return nc


def _colpiece(W, j):
    blk = W[:, 512 * j:512 * (j + 1)]
    return blk.reshape(8, 128, 512).transpose(1, 0, 2).reshape(128, 4096)


def _vec(a):
    return a.reshape(L, 8, 128).transpose(2, 0, 1).reshape(128, L * 8)


def prep_shared(inp):
    f = np.float32
    w_in = np.asarray(inp["w_in"], f)
    w_pa = np.asarray(inp["w_pa"], f)
    w_pb = np.asarray(inp["w_pb"], f)
    w_out = np.asarray(inp["w_out"], f)
    w_ff1 = np.asarray(inp["w_ff1"], f)
    w_ff2 = np.asarray(inp["w_ff2"], f)
    w_ada = np.asarray(inp["w_ada"], f)
    wst = np.empty((L, PIECES_PER_LAYER, 128, 4096), f)
    wada = np.empty((L, ADA_PIECES, 128, 4096), f)
    for l in range(L):
        ps = []
        for j in (4, 6, 5, 7, 2, 3, 0, 1):
            ps.append(_colpiece(w_in[l], j))
        for hf in range(2):
            ps.append(_colpiece(w_pa[l], hf))
            ps.append(_colpiece(w_pb[l], hf))
            ps.append(_colpiece(w_in[l], 8 + hf))
            ps.append(_colpiece(w_in[l], 10 + hf))
        ps.append(_colpiece(w_out[l], 0))
        ps.append(_colpiece(w_out[l], 1))
        for hh in range(2):
            for j in range(4):
                ps.append(_colpiece(w_ff1[l], hh * 4 + j))
            for q in range(4):
                blk = w_ff2[l][hh * 2048:(hh + 1) * 2048, q * 256:(q + 1) * 256]
                ps.append(blk.reshape(16, 128, 256).transpose(1, 0, 2).reshape(128, 4096))
        assert len(ps) == PIECES_PER_LAYER
        for i, p in enumerate(ps):
            wst[l, i] = p
        for j in range(ADA_PIECES):
            wada[l, j] = _colpiece(w_ada[l], j)
    params = np.zeros((128, NPARAM), f)

    def put(name, arr):
        params[:, OFF[name]:OFF[name] + arr.shape[1]] = arr
    put("n1g", _vec(np.asarray(inp["norm1_g"], f)))
    put("n2g", _vec(np.asarray(inp["norm2_g"], f)))
    put("alg", _vec(np.asarray(inp["a_ln_g"], f)))
    put("alb", _vec(np.asarray(inp["a_ln_b"], f)))
    put("cvb", _vec(np.asarray(inp["b_conv_b"], f)))
    put("blg", _vec(np.asarray(inp["b_ln_g"], f)))
    put("blb", _vec(np.asarray(inp["b_ln_b"], f)))
    put("fg", np.asarray(inp["final_g"], f).reshape(8, 128).T)
    put("bada", np.asarray(inp["b_ada"], f).reshape(L, 48, 128).transpose(2, 0, 1).reshape(128, L * 48))
    put("cvw", np.asarray(inp["b_conv_w"], f).reshape(L, CW, 8, 128).transpose(3, 0, 2, 1).reshape(128, L * 8 * CW))
    a_ws = np.asarray(inp["a_ws"], f)
    wsT = np.ascontiguousarray(a_ws.transpose(3, 0, 1, 2).reshape(128, L * 8 * 128))
    mask = np.ascontiguousarray(np.tril(np.ones((128, 128), f)).T)
    bsrow = np.ascontiguousarray(np.asarray(inp["a_bs"], f).reshape(1, L * 8 * 128))
    return {
        "params": params,
        "wada": wada.reshape(L * ADA_PIECES * 128, 4096),
        "wst": wst.reshape(L * PIECES_PER_LAYER * 128, 4096),
        "wsT": wsT, "mask": mask, "bsrow": bsrow, "ident": np.eye(128, dtype=f),
    }


_NC_CACHE = {}


def kernel(**inputs):
    x = np.asarray(inputs["x"], np.float32)
    c = np.asarray(inputs["c"], np.float32)
    B, T, Dm = x.shape
    assert (B, T, Dm) == (16, SEQ, D)
    shared = prep_shared(inputs)
    per = B // NCORES
    in_maps = []
    for i in range(NCORES):
        xs = x[i * per:(i + 1) * per].reshape(per * T, D)
        m = dict(shared)
        m["xT"] = np.ascontiguousarray(xs.T)
        m["cT"] = np.ascontiguousarray(c[i * per:(i + 1) * per].reshape(per, 8, 128).transpose(2, 1, 0).reshape(128, 16))
        in_maps.append(m)
    if "nc" not in _NC_CACHE:
        _NC_CACHE["nc"] = build_nc()
    nc = _NC_CACHE["nc"]
    res = run_bass_kernel_spmd(nc, in_maps, core_ids=list(range(NCORES)))
    out = np.empty((B, T, D), np.float32)
    for i in range(NCORES):
        o = np.asarray(res.results[i]["outT"])
        out[i * per:(i + 1) * per] = o.T.reshape(per, T, D)
    return out
```

```python
import numpy as np
import concourse.bass as bass
import concourse.mybir as mybir
from concourse.bass_utils import run_bass_kernel_spmd

F32 = mybir.dt.float32
BF16 = mybir.dt.bfloat16
AF = mybir.ActivationFunctionType
ALU = mybir.AluOpType

L = 4
D = 1024
KC = 8
SEQ = 4096
NCORES = 8
NT = 1024
ST = 512
NSUB = NT // ST
CW = 31
HALO = CW - 1
EPS = 1e-6
NSLOT = 6
K_DB = 12
K_PE = 24
PIECES_PER_LAYER = 34
ADA_PIECES = 12

OFF = {}
_o = 0
for _name, _n in [("n1g", L * 8), ("n2g", L * 8), ("alg", L * 8), ("alb", L * 8), ("cvb", L * 8),
                  ("blg", L * 8), ("blb", L * 8), ("fg", 8), ("bada", L * 48), ("cvw", L * 8 * CW)]:
    OFF[_name] = _o
    _o += _n
NPARAM = _o


class Buf:
    __slots__ = ("name", "last_w", "reads")

    def __init__(self, name):
        self.name = name
        self.last_w = None
        self.reads = []


class Prog:
    ENGS = ("pe", "act", "dve", "pool", "sp")

    def __init__(self):
        self.q = {e: [] for e in self.ENGS}
        self.count = {}
        self.waited = {e: {} for e in self.ENGS}

    def _deps(self, eng, reads, writes):
        deps = {}

        def add(tok, kind):
            if tok is None:
                return
            key, val = tok
            if key == eng:
                if eng == "pe":
                    return
                if kind != "raw":
                    return
            if deps.get(key, 0) < val:
                deps[key] = val

        for b in reads:
            add(b.last_w, "raw")
        for b in writes:
            add(b.last_w, "waw")
            for t in b.reads:
                add(t, "war")
        out = []
        w = self.waited[eng]
        for key, val in deps.items():
            if w.get(key, 0) < val:
                w[key] = val
                out.append((key, val))
        return out

    def op(self, eng, fn, reads=(), writes=()):
        waits = self._deps(eng, reads, writes)
        self.count[eng] = self.count.get(eng, 0) + 1
        tok = (eng, self.count[eng])
        self.q[eng].append((waits, fn, (eng, 1)))
        for b in reads:
            b.reads.append(tok)
        for b in writes:
            b.last_w = tok
            b.reads = []
        return tok

    def dma(self, eng, semkey, fn, reads=(), writes=()):
        waits = self._deps(eng, reads, writes)
        self.count[semkey] = self.count.get(semkey, 0) + 16
        tok = (semkey, self.count[semkey])
        self.q[eng].append((waits, fn, (semkey, 16)))
        for b in reads:
            b.reads.append(tok)
        for b in writes:
            b.last_w = tok
            b.reads = []
        return tok

    def dma_group(self, eng, semkey, items):
        toks = []
        allw = []
        for fn, reads, writes in items:
            self.dma(eng, semkey, fn, reads=reads, writes=writes)
            allw += list(writes)
        final = (semkey, self.count[semkey])
        for b in allw:
            b.last_w = final
        return final

    def final_wait(self, eng, toks):
        self.q[eng].append(([t for t in toks], None, None))


def bc_mid(ap2d, reps):
    a = ap2d.ap
    return bass.AP(ap2d.tensor, ap2d.offset, [[a[0][0], a[0][1]], [0, reps], [a[-1][0], a[-1][1]]])


def bc_last(ap2d, reps):
    a = ap2d.ap
    return bass.AP(ap2d.tensor, ap2d.offset, [[a[0][0], a[0][1]], [a[-1][0], a[-1][1]], [0, reps]])


def build_nc(n_tiles=8, n_layers=L, tiles_per_seq=4):
    nc = bass.Bass("TRN2", target_bir_lowering=False)
    NTOK = n_tiles * NT
    xT = nc.dram_tensor("xT", [D, NTOK], F32, kind="ExternalInput").ap()
    outT = nc.dram_tensor("outT", [D, NTOK], F32, kind="ExternalOutput").ap()
    params_d = nc.dram_tensor("params", [128, NPARAM], F32, kind="ExternalInput").ap()
    cT_d = nc.dram_tensor("cT", [128, 16], F32, kind="ExternalInput").ap()
    wada_d = nc.dram_tensor("wada", [L * ADA_PIECES * 128, 4096], F32, kind="ExternalInput").ap()
    wst_d = nc.dram_tensor("wst", [L * PIECES_PER_LAYER * 128, 4096], F32, kind="ExternalInput").ap()
    wt_d = nc.dram_tensor("wsT", [128, L * 8 * 128], F32, kind="ExternalInput").ap()
    mask_d = nc.dram_tensor("mask", [128, 128], F32, kind="ExternalInput").ap()
    ident_d = nc.dram_tensor("ident", [128, 128], F32, kind="ExternalInput").ap()
    bs_d = nc.dram_tensor("bsrow", [1, L * 8 * 128], F32, kind="ExternalInput")

    P = Prog()
    n_seq = n_tiles // tiles_per_seq

    from contextlib import ExitStack
    with ExitStack() as es:
        def sb(name, shape, dt):
            return es.enter_context(nc.sbuf_tensor(name, shape, dt))

        def sem(name):
            return es.enter_context(nc.semaphore(name))

        X = sb("X", [128, KC, NT], F32)
        A = sb("A", [128, KC, NT], BF16)
        U = sb("U", [128, 24576], BF16)
        ring = [sb(f"ring{i}", [128, 4096], BF16) for i in range(NSLOT)]
        PRM = sb("PRM", [128, NPARAM], F32)
        MOD = sb("MOD", [128, L * 48 * 2], F32)
        GS = sb("GS", [128, L * 2 * 8 * 2], F32)
        WMT = sb("WMT", [128, L * 8 * 128], BF16)
        CB = sb("CB", [128, 8 * 128], F32)
        MASK = sb("MASK", [128, 128], F32)
        ONES_S = sb("ONES_S", [128, 128], BF16)
        ONES_1 = sb("ONES_1", [128, 128], BF16)
        EPSB = sb("EPSB", [128, 1], F32)
        CIN = sb("CIN", [128, 16], F32)
        CSG = sb("CSG", [128, 16], F32)
        CACT = sb("CACT", [128, 16], BF16)
        ZT = sb("ZT", [128, L * 8 * HALO], BF16)
        ZB = [sb(f"ZB{i}", [128, HALO + ST], BF16) for i in range(3)]
        SQ = [sb(f"SQ{i}", [128, ST], BF16) for i in range(2)]
        FT = [sb(f"FT{i}", [128, ST], F32) for i in range(5)]
        ST1 = [sb(f"ST1_{i}", [128, ST], F32) for i in range(1)]
        RSTD = [sb(f"RSTD{i}", [128, ST], F32) for i in range(2)]
        MU = [sb(f"MU{i}", [128, ST], F32) for i in range(1)]
        VH = [sb(f"VH{i}", [128, 4, D], BF16) for i in range(1)]
        BNS = [sb(f"BNS{i}", [128, 12], F32) for i in range(2)]
        MV = [sb(f"MV{i}", [128, 2], F32) for i in range(2)]
        VT = [sb(f"VT{i}", [128, 2], F32) for i in range(2)]
        DG = sb("DG", [128, (K_PE + K_DB) * 128], BF16)

        banks = [es.enter_context(nc.psum_tensor(f"bank{i}", [128, ST], F32)) for i in range(8)]

        UF = U[:, 0:16384].bitcast(F32)

        def ZCa(cc, s):
            o = (s * KC + cc) * ST
            return UF[:, o:o + ST]

        def Ta(h, s):
            o = (s * KC + h) * ST
            return U[:, o:o + ST]

        def MGa(oc, s):
            o = 8192 + (s * KC + oc) * ST
            return U[:, o:o + ST]
        Qv = U[:, 16384:24576].rearrange("p (k t) -> p k t", k=KC)
        Tv = U[:, 0:8192].rearrange("p (k t) -> p k t", k=KC)
        MGv = U[:, 8192:16384].rearrange("p (k t) -> p k t", k=KC)
        Rv = U[:, 0:16384].rearrange("p (k t) -> p k t", k=16)
        WTF = U[:, 0:8192].bitcast(F32)

        sems = {e: sem(f"s_{e}") for e in ("pe", "act", "dve", "pool")}
        for i in range(NSLOT):
            sems[f"ring{i}"] = sem(f"s_ring{i}")
        for k in ("ldx", "ldp", "st0", "st1", "st2", "st3", "st4", "bsd", "ldi"):
            sems[k] = sem(f"s_{k}")

        bX = [[Buf(f"X{k}_{s}") for s in range(NSUB)] for k in range(KC)]
        bA = [[Buf(f"A{k}_{s}") for s in range(NSUB)] for k in range(KC)]
        bUg = [Buf(f"U{g}") for g in range(48)]
        bU_ZC = [bUg[cc * 4:cc * 4 + 4] for cc in range(KC)]
        bU_ZCs = [[bUg[2 * (s * KC + cc):2 * (s * KC + cc) + 2] for s in range(NSUB)] for cc in range(KC)]
        bQ = [[[bUg[32 + k * 2 + s]] for s in range(NSUB)] for k in range(KC)]
        bT = [[[bUg[s * KC + k]] for s in range(NSUB)] for k in range(KC)]
        bMG = [[[bUg[16 + s * KC + k]] for s in range(NSUB)] for k in range(KC)]
        bR = [[[bUg[k * 2 + s]] for s in range(NSUB)] for k in range(16)]
        bring = [Buf(f"ring{i}") for i in range(NSLOT)]
        bbank = [Buf(f"bank{i}") for i in range(8)]
        bPRM = Buf("PRM"); bMOD = Buf("MOD"); bGS = Buf("GS"); bWMT = Buf("WMT"); bCB = Buf("CB")
        bMASK = Buf("MASK"); bCONST = Buf("CONST"); bCIN = Buf("CIN"); bCSG = Buf("CSG"); bCACT = Buf("CACT")
        bZT = [[Buf(f"ZT{l}_{k}") for k in range(KC)] for l in range(L)]
        bZB = [Buf(f"ZB{i}") for i in range(3)]
        bSQ = [Buf(f"SQ{i}") for i in range(2)]
        bFT = [Buf(f"FT{i}") for i in range(5)]
        bST1 = [Buf(f"ST1{i}") for i in range(1)]
        bRSTD = [Buf(f"RSTD{i}") for i in range(2)]
        bMU = [Buf(f"MU{i}") for i in range(1)]
        bVH = [[Buf(f"VH{i}_{c}") for c in range(4)] for i in range(1)]
        bBNS = [Buf(f"BNS{i}") for i in range(2)]
        bMV = [Buf(f"MV{i}") for i in range(2)]
        bVT = [Buf(f"VT{i}") for i in range(2)]
        bDG = [Buf(f"DG{k}") for k in range(K_PE + K_DB)]
        bWTF = bUg[0:16]

        rot = {}

        def nxt(name, n):
            i = rot.get(name, 0)
            rot[name] = (i + 1) % n
            return i

        piece_src = []
        for l in range(L):
            for j in range(ADA_PIECES):
                r0 = (l * ADA_PIECES + j) * 128
                piece_src.append(wada_d[r0:r0 + 128, :])
        n_ada = len(piece_src)
        for t in range(n_tiles):
            for l in range(n_layers):
                for j in range(PIECES_PER_LAYER):
                    r0 = (l * PIECES_PER_LAYER + j) * 128
                    piece_src.append(wst_d[r0:r0 + 128, :])
        issued = [0]

        def issue_piece(n):
            assert n == issued[0]
            if n >= len(piece_src):
                return
            slot = n % NSLOT
            src = piece_src[n]
            P.dma("pool", f"ring{slot}",
                  lambda e, slot=slot, src=src: e.dma_start(out=ring[slot][:], in_=src),
                  writes=[bring[slot]])
            issued[0] += 1

        def release_piece(n):
            issue_piece_target = n + NSLOT
            while issued[0] <= issue_piece_target and issued[0] < len(piece_src):
                issue_piece(issued[0])

        for n in range(NSLOT):
            issue_piece(n)

        def slot_of(n):
            assert n < issued[0], (n, issued[0])
            return ring[n % NSLOT], bring[n % NSLOT]

        def mm_group(bank_i, pairs, reads, out_ap=None, writes=None):
            o = out_ap if out_ap is not None else banks[bank_i][:]
            n = len(pairs)

            def fn(e, o=o, pairs=pairs, n=n):
                ins = None
                for i, (lt, rh) in enumerate(pairs):
                    ins = e.matmul(o, lt, rh, start=(i == 0), stop=(i == n - 1))
                return ins
            return P.op("pe", fn, reads=reads, writes=writes if writes is not None else [bbank[bank_i]])

        def prm(name, idx):
            o = OFF[name] + idx
            return PRM[:, o:o + 1]

        def gs_ap(l, which, kc, s):
            o = ((l * 2 + which) * 8 + kc) * 2 + s
            return GS[:, o:o + 1]

        def mod_ap(l, m, kc, s):
            o = (l * 48 + m * 8 + kc) * 2 + s
            return MOD[:, o:o + 1]

        P.dma_group("sp", "ldp", [
            (lambda e: e.dma_start(out=PRM[:], in_=params_d[:, :]), [], [bPRM]),
            (lambda e: e.dma_start(out=CIN[:], in_=cT_d[:, :]), [], [bCIN]),
            (lambda e: e.dma_start(out=MASK[:], in_=mask_d[:, :]), [], [bMASK]),
            (lambda e: e.dma_start(out=WTF, in_=wt_d[:, :]), [], bWTF),
        ])
        P.op("dve", lambda e: e.memset(ONES_S[:], 1.0 / 1024.0), writes=[bCONST])
        P.op("dve", lambda e: e.memset(ONES_1[:], 1.0), writes=[bCONST])
        P.op("dve", lambda e: e.memset(EPSB[:], EPS), writes=[bCONST])
        P.op("act", lambda e: e.activation(out=CSG[:], in_=CIN[:], func=AF.Sigmoid), reads=[bCIN], writes=[bCSG])
        P.op("dve", lambda e: e.tensor_tensor(out=CACT[:], in0=CIN[:], in1=CSG[:], op=ALU.mult),
             reads=[bCIN, bCSG], writes=[bCACT])
        pn = 0
        for l in range(L):
            for j in range(ADA_PIECES):
                slot, bslot = slot_of(pn)
                bk = nxt("bank", 6)
                first = True
                for ocl in range(4):
                    pairs = [(slot[:, kc * 512 + ocl * 128: kc * 512 + (ocl + 1) * 128], CACT[:, kc * 2:(kc + 1) * 2])
                             for kc in range(KC)]
                    mm_group(bk, pairs, reads=[bslot, bCACT], out_ap=banks[bk][:, ocl * 2:(ocl + 1) * 2],
                             writes=[bbank[bk]] if first else [])
                    if not first:
                        bbank[bk].last_w = ("pe", P.count["pe"])
                    first = False
                o = (l * 48 + 4 * j) * 2
                bo = OFF["bada"] + l * 48 + 4 * j
                P.op("dve", lambda e, bk=bk, o=o, bo=bo: e.tensor_tensor(
                    out=MOD[:, o:o + 8].rearrange("p (c s) -> p c s", s=2),
                    in0=banks[bk][:, 0:8].rearrange("p (c s) -> p c s", s=2),
                    in1=bc_last(PRM[:, bo:bo + 4], 2), op=ALU.add),
                    reads=[bbank[bk], bPRM], writes=[bMOD])
                release_piece(pn)
                pn += 1
        for l in range(L):
            for which, m, gname in ((0, 1, "n1g"), (1, 4, "n2g")):
                o = (l * 2 + which) * 16
                mo = (l * 48 + m * 8) * 2
                go = OFF[gname] + l * 8
                P.op("dve", lambda e, o=o, mo=mo: e.tensor_scalar(
                    out=GS[:, o:o + 16], in0=MOD[:, mo:mo + 16], scalar1=1.0, scalar2=None, op0=ALU.add),
                    reads=[bMOD], writes=[bGS])
                P.op("dve", lambda e, o=o, go=go: e.tensor_tensor(
                    out=GS[:, o:o + 16].rearrange("p (c s) -> p c s", s=2),
                    in0=GS[:, o:o + 16].rearrange("p (c s) -> p c s", s=2),
                    in1=bc_last(PRM[:, go:go + 8], 2), op=ALU.mult),
                    reads=[bGS, bPRM], writes=[bGS])
        P.op("dve", lambda e: e.tensor_tensor(
            out=WMT[:].rearrange("p (g i) -> p g i", i=128),
            in0=WTF.rearrange("p (g i) -> p g i", i=128),
            in1=bc_mid(MASK[:], L * 8), op=ALU.mult),
            reads=bWTF + [bMASK], writes=[bWMT])
        P.dma("sp", "ldi", lambda e: e.dma_start(out=MASK[:], in_=ident_d[:, :]), writes=[bMASK])
        def gate_prep(l):
            P.dma("sp", "bsd", lambda e: e.dma_start(
                out=CB[:], in_=bass.AP(bs_d, l * 1024, [[0, 128], [1, 1024]])), writes=[bCB])
            for hf in range(2):
                bk = nxt("bank", 6)
                c0 = l * 1024 + hf * 512
                mm_group(bk, [(ONES_1[:], WMT[:, c0:c0 + 512])], reads=[bWMT, bCONST])
                for hh in range(4):
                    h = hf * 4 + hh
                    cc0 = h * 128
                    bidx = OFF["alb"] + l * 8 + h
                    P.op("dve", lambda e, bk=bk, hh=hh, cc0=cc0, bidx=bidx: e.scalar_tensor_tensor(
                        out=CB[:, cc0:cc0 + 128], in0=banks[bk][:, hh * 128:(hh + 1) * 128],
                        scalar=PRM[:, bidx:bidx + 1], in1=CB[:, cc0:cc0 + 128], op0=ALU.mult, op1=ALU.add),
                        reads=[bbank[bk], bPRM, bCB], writes=[bCB])

        def rms_stats(src_bufs_fn, src_ap_fn, s):
            bk = nxt("bank", 6)
            for kc in range(KC):
                qi = nxt("SQ", 2)
                P.op("act", lambda e, kc=kc, qi=qi: e.activation(out=SQ[qi][:], in_=src_ap_fn(kc, s), func=AF.Square),
                     reads=[src_bufs_fn(kc, s)], writes=[bSQ[qi]])

                def fn(e, kc=kc, qi=qi, bk=bk):
                    return e.matmul(banks[bk][:], ONES_S[:], SQ[qi][:], start=(kc == 0), stop=(kc == KC - 1))
                P.op("pe", fn, reads=[bSQ[qi], bCONST], writes=[bbank[bk]] if kc == 0 else [])
            bbank[bk].last_w = ("pe", P.count["pe"])
            si = nxt("ST1", 1)
            P.op("act", lambda e, bk=bk, si=si: e.activation(
                out=ST1[si][:], in_=banks[bk][:], func=AF.Sqrt, bias=EPSB[:, 0:1]),
                reads=[bbank[bk], bCONST], writes=[bST1[si]])
            ri = nxt("RSTD", 2)
            P.op("dve", lambda e, si=si, ri=ri: e.reciprocal(out=RSTD[ri][:], in_=ST1[si][:]),
                 reads=[bST1[si]], writes=[bRSTD[ri]])
            return ri

        def norm_gen(l, which, sq, s):
            shm = 0 if which == 0 else 3
            ri = rms_stats(lambda kc, s: bX[kc][s], lambda kc, s: X[:, kc, s * ST:(s + 1) * ST], s)
            yield
            for kc in range(KC):
                fi = nxt("FT", 5)
                P.op("dve", lambda e, kc=kc, s=s, fi=fi, ri=ri: e.tensor_tensor(
                    out=FT[fi][:], in0=X[:, kc, s * ST:(s + 1) * ST], in1=RSTD[ri][:], op=ALU.mult),
                    reads=[bX[kc][s], bRSTD[ri]], writes=[bFT[fi]])
                P.op("act", lambda e, kc=kc, s=s, fi=fi: e.activation(
                    out=A[:, kc, s * ST:(s + 1) * ST], in_=FT[fi][:], func=AF.Identity,
                    scale=gs_ap(l, which, kc, sq), bias=mod_ap(l, shm, kc, sq)),
                    reads=[bFT[fi], bGS, bMOD], writes=[bA[kc][s]])
                if kc % 2 == 1:
                    yield

        def run(g):
            for _ in g:
                pass

        def interleave(*gens):
            gens = list(gens)
            while gens:
                for g in list(gens):
                    try:
                        next(g)
                    except StopIteration:
                        gens.remove(g)

        def A_all(s):
            return [bA[kc][s] for kc in range(KC)]

        def conv_pe_gen(l, pn0, s, zero_halo):
            zis = {}
            tsl = slice(s * ST, (s + 1) * ST)

            def ag(cc):
                half, cl = divmod(cc, 4)
                sa, bsa = slot_of(pn0 + 2 * half)
                sg, bsg = slot_of(pn0 + 2 * half + 1)
                zi = nxt("ZB", 3)
                zis[cc] = zi
                zo = (l * 8 + cc) * HALO
                if zero_halo:
                    P.op("act", lambda e, zi=zi: e.memzero(ZB[zi][:, 0:HALO]), writes=[bZB[zi]])
                else:
                    P.op("act", lambda e, zi=zi, zo=zo: e.copy(out=ZB[zi][:, 0:HALO], in_=ZT[:, zo:zo + HALO]),
                         reads=[bZT[l][cc]], writes=[bZB[zi]])
                ba = nxt("bank", 6)
                mm_group(ba, [(sa[:, kc * 512 + cl * 128: kc * 512 + (cl + 1) * 128], A[:, kc, tsl])
                              for kc in range(KC)], reads=[bsa] + A_all(s))
                bg = nxt("bank", 6)
                mm_group(bg, [(sg[:, kc * 512 + cl * 128: kc * 512 + (cl + 1) * 128], A[:, kc, tsl])
                              for kc in range(KC)], reads=[bsg] + A_all(s))
                fi = nxt("FT", 5)
                P.op("act", lambda e, bg=bg, fi=fi: e.activation(out=FT[fi][:], in_=banks[bg][:], func=AF.Sigmoid),
                     reads=[bbank[bg]], writes=[bFT[fi]])
                P.op("dve", lambda e, ba=ba, fi=fi, zi=zi: e.tensor_tensor(
                    out=ZB[zi][:, HALO:HALO + ST], in0=banks[ba][:], in1=FT[fi][:], op=ALU.mult),
                    reads=[bbank[ba], bFT[fi]], writes=[bZB[zi]])

            dgs = {}

            def dgi(par, k):
                return par * K_DB + k if k < K_DB else 2 * K_DB + (k - K_DB)

            def diag(cc, lo, hi):
                if lo == 0:
                    dgs[cc] = nxt("DGP", 2)
                par = dgs[cc]
                wo = OFF["cvw"] + (l * 8 + cc) * CW
                for k in range(lo, hi):
                    d0 = dgi(par, k) * 128
                    P.op("act", lambda e, k=k, wo=wo, d0=d0: e.activation(
                        out=DG[:, d0:d0 + 128], in_=MASK[:], func=AF.Identity, scale=PRM[:, wo + k:wo + k + 1]),
                        reads=[bMASK, bPRM], writes=[bDG[dgi(par, k)]])
                if lo == 0:
                    zi = zis[cc]
                    zo = (l * 8 + cc) * HALO
                    P.op("act", lambda e, zi=zi, zo=zo: e.copy(out=ZT[:, zo:zo + HALO], in_=ZB[zi][:, ST:ST + HALO]),
                         reads=[bZB[zi]], writes=[bZT[l][cc]])

            def taps(cc):
                zi = zis[cc]
                par = dgs[cc]
                wo = OFF["cvw"] + (l * 8 + cc) * CW
                bo = OFF["cvb"] + l * 8 + cc
                bk = nxt("bank", 6)
                for k in range(K_PE):
                    d0 = dgi(par, k) * 128
                    P.op("pe", lambda e, k=k, zi=zi, bk=bk, d0=d0: e.matmul(
                        banks[bk][:], DG[:, d0:d0 + 128], ZB[zi][:, k:k + ST],
                        start=(k == 0), stop=(k == K_PE - 1)),
                        reads=[bDG[dgi(par, k)], bZB[zi]], writes=[bbank[bk]] if k == 0 else [])
                bbank[bk].last_w = ("pe", P.count["pe"])
                P.op("dve", lambda e, cc=cc, bk=bk, bo=bo: e.tensor_scalar(
                    out=ZCa(cc, s), in0=banks[bk][:], scalar1=PRM[:, bo:bo + 1], scalar2=None,
                    op0=ALU.add), reads=[bbank[bk], bPRM], writes=bU_ZCs[cc][s])
                for k in range(K_PE, CW):
                    P.op("dve", lambda e, cc=cc, zi=zi, wo=wo, k=k: e.scalar_tensor_tensor(
                        out=ZCa(cc, s), in0=ZB[zi][:, k:k + ST], scalar=PRM[:, wo + k:wo + k + 1],
                        in1=ZCa(cc, s), op0=ALU.mult, op1=ALU.add),
                        reads=[bZB[zi], bPRM] + bU_ZCs[cc][s], writes=bU_ZCs[cc][s])

            ag(0)
            diag(0, 0, K_DB)
            diag(0, K_DB, K_PE)
            for cc in range(KC):
                if cc + 1 < KC:
                    ag(cc + 1)
                    diag(cc + 1, 0, K_DB)
                taps(cc)
                if cc + 1 < KC:
                    diag(cc + 1, K_DB, K_PE)
                yield

        def ln_gen(l, s):
            bm, bx = 6, 7
            for cc in range(KC):
                q1 = nxt("SQ", 2)
                P.op("act", lambda e, cc=cc, q1=q1: e.copy(out=SQ[q1][:], in_=ZCa(cc, s)),
                     reads=bU_ZCs[cc][s], writes=[bSQ[q1]])
                P.op("pe", lambda e, cc=cc, q1=q1, bm=bm: e.matmul(
                    banks[bm][:], ONES_S[:], SQ[q1][:], start=(cc == 0), stop=(cc == KC - 1)),
                    reads=[bSQ[q1], bCONST], writes=[bbank[bm]] if cc == 0 else [])
                q2 = nxt("SQ", 2)
                P.op("act", lambda e, cc=cc, q2=q2: e.activation(
                    out=SQ[q2][:], in_=ZCa(cc, s), func=AF.Square),
                    reads=bU_ZCs[cc][s], writes=[bSQ[q2]])
                P.op("pe", lambda e, cc=cc, q2=q2, bx=bx: e.matmul(
                    banks[bx][:], ONES_S[:], SQ[q2][:], start=(cc == 0), stop=(cc == KC - 1)),
                    reads=[bSQ[q2], bCONST], writes=[bbank[bx]] if cc == 0 else [])
                if cc % 2 == 1:
                    yield
            bbank[bm].last_w = ("pe", P.count["pe"])
            bbank[bx].last_w = ("pe", P.count["pe"])
            mi = nxt("MU", 1)
            P.op("dve", lambda e, bm=bm, mi=mi: e.tensor_copy(out=MU[mi][:], in_=banks[bm][:]),
                 reads=[bbank[bm]], writes=[bMU[mi]])
            si = nxt("ST1", 1)
            P.op("dve", lambda e, mi=mi, si=si: e.tensor_tensor(out=ST1[si][:], in0=MU[mi][:], in1=MU[mi][:], op=ALU.mult),
                 reads=[bMU[mi]], writes=[bST1[si]])
            P.op("dve", lambda e, bx=bx, si=si: e.scalar_tensor_tensor(
                out=ST1[si][:], in0=banks[bx][:], scalar=EPS, in1=ST1[si][:], op0=ALU.add, op1=ALU.subtract),
                reads=[bbank[bx], bST1[si]], writes=[bST1[si]])
            P.op("act", lambda e, si=si: e.activation(out=ST1[si][:], in_=ST1[si][:], func=AF.Sqrt),
                 reads=[bST1[si]], writes=[bST1[si]])
            ri = nxt("RSTD", 2)
            P.op("dve", lambda e, si=si, ri=ri: e.reciprocal(out=RSTD[ri][:], in_=ST1[si][:]),
                 reads=[bST1[si]], writes=[bRSTD[ri]])
            yield
            for cc in range(KC):
                f1 = nxt("FT", 5)
                P.op("dve", lambda e, cc=cc, f1=f1, mi=mi: e.tensor_tensor(
                    out=FT[f1][:], in0=ZCa(cc, s), in1=MU[mi][:], op=ALU.subtract),
                    reads=bU_ZCs[cc][s] + [bMU[mi]], writes=[bFT[f1]])
                f2 = nxt("FT", 5)
                P.op("dve", lambda e, f1=f1, f2=f2, ri=ri: e.tensor_tensor(
                    out=FT[f2][:], in0=FT[f1][:], in1=RSTD[ri][:], op=ALU.mult),
                    reads=[bFT[f1], bRSTD[ri]], writes=[bFT[f2]])
                gi = OFF["blg"] + l * 8 + cc
                bi = OFF["blb"] + l * 8 + cc
                f3 = nxt("FT", 5)
                P.op("act", lambda e, f2=f2, f3=f3, gi=gi, bi=bi: e.activation(
                    out=FT[f3][:], in_=FT[f2][:], func=AF.Sigmoid, scale=PRM[:, gi:gi + 1], bias=PRM[:, bi:bi + 1]),
                    reads=[bFT[f2], bPRM], writes=[bFT[f3]])
                P.op("act", lambda e, f2=f2, f1=f1, gi=gi, bi=bi: e.activation(
                    out=FT[f1][:], in_=FT[f2][:], func=AF.Identity, scale=PRM[:, gi:gi + 1], bias=PRM[:, bi:bi + 1]),
                    reads=[bFT[f2], bPRM], writes=[bFT[f1]])
                P.op("dve", lambda e, cc=cc, f1=f1, f3=f3: e.tensor_tensor(
                    out=Qv[:, cc, s * ST:(s + 1) * ST], in0=FT[f1][:], in1=FT[f3][:], op=ALU.mult),
                    reads=[bFT[f1], bFT[f3]], writes=bQ[cc][s])
                yield

        def gate_gen(l, pn0, s):
            sv = [slot_of(pn0), slot_of(pn0 + 1)]
            su = [slot_of(pn0 + 2), slot_of(pn0 + 3)]
            if True:
                for c in range(4):
                    tok0 = s * ST + c * 128
                    pv = []
                    for hv in range(2):
                        bk = nxt("bank", 6)
                        mm_group(bk, [(A[:, kc, tok0:tok0 + 128], sv[hv][0][:, kc * 512:(kc + 1) * 512]) for kc in range(KC)],
                                 reads=[sv[hv][1]] + A_all(s))
                        pv.append(bk)
                    bi = nxt("BNS", 2)
                    P.op("dve", lambda e, bi=bi, b0=pv[0]: e.bn_stats(out=BNS[bi][:, 0:6], in_=banks[b0][:]),
                         reads=[bbank[pv[0]]], writes=[bBNS[bi]])
                    P.op("dve", lambda e, bi=bi, b1=pv[1]: e.bn_stats(out=BNS[bi][:, 6:12], in_=banks[b1][:]),
                         reads=[bbank[pv[1]], bBNS[bi]], writes=[bBNS[bi]])
                    P.op("dve", lambda e, bi=bi: e.bn_aggr(out=MV[bi][:], in_=BNS[bi][:]),
                         reads=[bBNS[bi]], writes=[bMV[bi]])
                    P.op("act", lambda e, bi=bi: e.activation(
                        out=VT[bi][:, 0:1], in_=MV[bi][:, 1:2], func=AF.Sqrt, bias=EPSB[:, 0:1]),
                        reads=[bMV[bi], bCONST], writes=[bVT[bi]])
                    P.op("dve", lambda e, bi=bi: e.reciprocal(out=VT[bi][:, 1:2], in_=VT[bi][:, 0:1]),
                         reads=[bVT[bi]], writes=[bVT[bi]])
                    for hv in range(2):
                        P.op("dve", lambda e, bi=bi, c=c, hv=hv, bk=pv[hv]: e.tensor_scalar(
                            out=VH[0][:, c, hv * 512:(hv + 1) * 512], in0=banks[bk][:],
                            scalar1=MV[bi][:, 0:1], scalar2=VT[bi][:, 1:2], op0=ALU.subtract, op1=ALU.mult),
                            reads=[bbank[pv[hv]], bMV[bi], bVT[bi]], writes=[bVH[0][c]])
                    yield
                for h in range(KC):
                    usl, busl = su[h // 4]
                    hl = h % 4
                    bu = nxt("bank", 6)
                    mm_group(bu, [(usl[:, kc * 512 + hl * 128: kc * 512 + (hl + 1) * 128], A[:, kc, s * ST:(s + 1) * ST])
                                  for kc in range(KC)], reads=[busl] + A_all(s))
                    bs0 = nxt("bank", 6)
                    wo = (l * 8 + h) * 128
                    first = True
                    for c in range(4):
                        mm_group(bs0, [(VH[0][:, c, h * 128:(h + 1) * 128], WMT[:, wo:wo + 128])],
                                 reads=[bVH[0][c], bWMT], out_ap=banks[bs0][:, c * 128:(c + 1) * 128],
                                 writes=[bbank[bs0]] if first else [])
                        first = False
                    bbank[bs0].last_w = ("pe", P.count["pe"])
                    fi = nxt("FT", 5)
                    gi = OFF["alg"] + l * 8 + h
                    P.op("dve", lambda e, bs0=bs0, fi=fi, gi=gi, h=h: e.scalar_tensor_tensor(
                        out=FT[fi][:].rearrange("p (c i) -> p c i", c=4),
                        in0=banks[bs0][:].rearrange("p (c i) -> p c i", c=4),
                        scalar=PRM[:, gi:gi + 1], in1=bc_mid(CB[:, h * 128:(h + 1) * 128], 4), op0=ALU.mult, op1=ALU.add),
                        reads=[bbank[bs0], bPRM, bCB], writes=[bFT[fi]])
                    P.op("dve", lambda e, bu=bu, fi=fi, h=h, s=s: e.tensor_tensor(
                        out=Ta(h, s), in0=banks[bu][:], in1=FT[fi][:], op=ALU.mult),
                        reads=[bbank[bu], bFT[fi]], writes=bT[h][s])
                    yield

        def merge_stage(l, pn0):
            for hf in range(2):
                spa, bpa = slot_of(pn0 + 4 * hf)
                spb, bpb = slot_of(pn0 + 4 * hf + 1)
                sga, bga = slot_of(pn0 + 4 * hf + 2)
                sgb, bgb = slot_of(pn0 + 4 * hf + 3)
                for s in range(NSUB):
                    for ocl in range(4):
                        oc = hf * 4 + ocl

                        def wcol(sl, kc, ocl=ocl):
                            return sl[:, kc * 512 + ocl * 128: kc * 512 + (ocl + 1) * 128]
                        tsl = slice(s * ST, (s + 1) * ST)
                        bga_k = nxt("bank", 6)
                        mm_group(bga_k, [(wcol(sga, kc), A[:, kc, tsl]) for kc in range(KC)], reads=[bga] + A_all(s))
                        bgb_k = nxt("bank", 6)
                        mm_group(bgb_k, [(wcol(sgb, kc), A[:, kc, tsl]) for kc in range(KC)], reads=[bgb] + A_all(s))
                        bya = nxt("bank", 6)
                        mm_group(bya, [(wcol(spa, kc), Ta(kc, s)) for kc in range(KC)],
                                 reads=[bpa] + [bT[kc][s][0] for kc in range(KC)])
                        byb = nxt("bank", 6)
                        mm_group(byb, [(wcol(spb, kc), Qv[:, kc, tsl]) for kc in range(KC)],
                                 reads=[bpb] + [bQ[kc][s][0] for kc in range(KC)])
                        f1 = nxt("FT", 5)
                        P.op("act", lambda e, b=bga_k, f1=f1: e.activation(out=FT[f1][:], in_=banks[b][:], func=AF.Sigmoid),
                             reads=[bbank[bga_k]], writes=[bFT[f1]])
                        f2 = nxt("FT", 5)
                        P.op("act", lambda e, b=bgb_k, f2=f2: e.activation(out=FT[f2][:], in_=banks[b][:], func=AF.Sigmoid),
                             reads=[bbank[bgb_k]], writes=[bFT[f2]])
                        P.op("dve", lambda e, b=bya, f1=f1: e.tensor_tensor(out=FT[f1][:], in0=banks[b][:], in1=FT[f1][:], op=ALU.mult),
                             reads=[bbank[bya], bFT[f1]], writes=[bFT[f1]])
                        P.op("dve", lambda e, b=byb, f2=f2: e.tensor_tensor(out=FT[f2][:], in0=banks[b][:], in1=FT[f2][:], op=ALU.mult),
                             reads=[bbank[byb], bFT[f2]], writes=[bFT[f2]])
                        P.op("dve", lambda e, f1=f1, f2=f2, oc=oc, s=s: e.tensor_tensor(
                            out=MGa(oc, s), in0=FT[f1][:], in1=FT[f2][:], op=ALU.add),
                            reads=[bFT[f1], bFT[f2]], writes=bMG[oc][s])
                for i in range(4):
                    release_piece(pn0 + 4 * hf + i)

        def out_gen(l, pn0, sq, s):
            tsl = slice(s * ST, (s + 1) * ST)
            for hf in range(2):
                so, bso = slot_of(pn0 + hf)
                for ocl in range(4):
                    oc = hf * 4 + ocl
                    bk = nxt("bank", 6)
                    mm_group(bk, [(so[:, kc * 512 + ocl * 128: kc * 512 + (ocl + 1) * 128], MGa(kc, s)) for kc in range(KC)],
                             reads=[bso] + [bMG[kc][s][0] for kc in range(KC)])
                    P.op("dve", lambda e, bk=bk, oc=oc: e.scalar_tensor_tensor(
                        out=X[:, oc, tsl], in0=banks[bk][:], scalar=mod_ap(l, 2, oc, sq), in1=X[:, oc, tsl],
                        op0=ALU.mult, op1=ALU.add),
                        reads=[bbank[bk], bMOD, bX[oc][s]], writes=[bX[oc][s]])
                    if ocl % 2 == 1:
                        yield

        def ff1_gen(p1, s_list):
            for s in s_list:
                tsl = slice(s * ST, (s + 1) * ST)
                for j in range(4):
                    sf, bsf = slot_of(p1 + j)
                    for hcl in range(4):
                        hc = j * 4 + hcl
                        bk = nxt("bank", 6)
                        mm_group(bk, [(sf[:, kc * 512 + hcl * 128: kc * 512 + (hcl + 1) * 128], A[:, kc, tsl]) for kc in range(KC)],
                                 reads=[bsf] + A_all(s))
                        fi = nxt("FT", 5)
                        P.op("act", lambda e, bk=bk, fi=fi: e.activation(out=FT[fi][:], in_=banks[bk][:], func=AF.Relu),
                             reads=[bbank[bk]], writes=[bFT[fi]])
                        P.op("act", lambda e, fi=fi, hc=hc, tsl=tsl: e.activation(
                            out=Rv[:, hc, tsl], in_=FT[fi][:], func=AF.Square),
                            reads=[bFT[fi]], writes=bR[hc][s])
                        if hcl % 2 == 1:
                            yield

        def ff2_gen(l, p2, sq, s_list):
            for s in s_list:
                tsl = slice(s * ST, (s + 1) * ST)
                for q in range(4):
                    s2, bs2 = slot_of(p2 + q)
                    for ocl in range(2):
                        oc = q * 2 + ocl
                        bk = nxt("bank", 6)
                        mm_group(bk, [(s2[:, kc * 256 + ocl * 128: kc * 256 + (ocl + 1) * 128], Rv[:, kc, tsl]) for kc in range(16)],
                                 reads=[bs2] + [bR[kc][s][0] for kc in range(16)])
                        P.op("dve", lambda e, bk=bk, oc=oc, tsl=tsl: e.scalar_tensor_tensor(
                            out=X[:, oc, tsl], in0=banks[bk][:], scalar=mod_ap(l, 5, oc, sq), in1=X[:, oc, tsl],
                            op0=ALU.mult, op1=ALU.add),
                            reads=[bbank[bk], bMOD, bX[oc][s]], writes=[bX[oc][s]])
                        yield

        def final_stage(t):
            for s in range(NSUB):
                ri = rms_stats(lambda kc, s: bX[kc][s], lambda kc, s: X[:, kc, s * ST:(s + 1) * ST], s)
                for kc in range(KC):
                    fi = nxt("FT", 5)
                    P.op("dve", lambda e, kc=kc, s=s, fi=fi, ri=ri: e.tensor_tensor(
                        out=FT[fi][:], in0=X[:, kc, s * ST:(s + 1) * ST], in1=RSTD[ri][:], op=ALU.mult),
                        reads=[bX[kc][s], bRSTD[ri]], writes=[bFT[fi]])
                    oi = nxt("FT", 5)
                    fo = OFF["fg"] + kc
                    P.op("act", lambda e, fi=fi, oi=oi, fo=fo: e.activation(
                        out=FT[oi][:], in_=FT[fi][:], func=AF.Identity, scale=PRM[:, fo:fo + 1]),
                        reads=[bFT[fi], bPRM], writes=[bFT[oi]])
                    c0 = t * NT + s * ST
                    P.dma("sp", f"st{oi}", lambda e, oi=oi, kc=kc, c0=c0: e.dma_start(
                        out=outT[kc * 128:(kc + 1) * 128, c0:c0 + ST], in_=FT[oi][:]),
                        reads=[bFT[oi]])

        pn = n_ada
        for t in range(n_tiles):
            sq = t // tiles_per_seq
            first_in_seq = (t % tiles_per_seq == 0)
            P.dma_group("sp", "ldx", [
                (lambda e, kc=kc, t=t: e.dma_start(out=X[:, kc, :], in_=xT[kc * 128:(kc + 1) * 128, t * NT:(t + 1) * NT]),
                 [], [bX[kc][0], bX[kc][1]]) for kc in range(KC)])
            norm0_done = False
            for l in range(n_layers):
                gate_prep(l)
                if not norm0_done:
                    run(norm_gen(l, 0, sq, 0))
                interleave(norm_gen(l, 0, sq, 1), conv_pe_gen(l, pn, 0, first_in_seq))
                interleave(conv_pe_gen(l, pn, 1, False), ln_gen(l, 0))
                for i in range(4):
                    release_piece(pn + i)
                run(gate_gen(l, pn + 4, 0))
                interleave(ln_gen(l, 1), gate_gen(l, pn + 4, 1))
                for i in range(4):
                    release_piece(pn + 4 + i)
                merge_stage(l, pn + 8)
                run(out_gen(l, pn + 16, sq, 0))
                interleave(out_gen(l, pn + 16, sq, 1), norm_gen(l, 1, sq, 0))
                release_piece(pn + 16)
                release_piece(pn + 17)
                interleave(norm_gen(l, 1, sq, 1), ff1_gen(pn + 18, [0]))
                run(ff1_gen(pn + 18, [1]))
                for i in range(4):
                    release_piece(pn + 18 + i)
                run(ff2_gen(l, pn + 22, sq, [0, 1]))
                for i in range(4):
                    release_piece(pn + 22 + i)
                run(ff1_gen(pn + 26, [0, 1]))
                for i in range(4):
                    release_piece(pn + 26 + i)
                run(ff2_gen(l, pn + 30, sq, [0]))
                if l + 1 < n_layers:
                    interleave(ff2_gen(l, pn + 30, sq, [1]), norm_gen(l + 1, 0, sq, 0))
                    norm0_done = True
                else:
                    run(ff2_gen(l, pn + 30, sq, [1]))
                    norm0_done = False
                for i in range(4):
                    release_piece(pn + 30 + i)
                pn += PIECES_PER_LAYER
            final_stage(t)

        P.final_wait("sp", [(f"st{i}", P.count.get(f"st{i}", 0)) for i in range(5) if P.count.get(f"st{i}", 0)])

        with nc.Block() as block:
            def replay(engname):
                def run(e):
                    for waits, fn, sig in P.q[engname]:
                        for key, val in waits:
                            e.wait_ge(sems[key], val)
                        if fn is None:
                            continue
                        ins = fn(e)
                        ins.then_inc(sems[sig[0]], sig[1])
                return run

            block.tensor(replay("pe"))
            block.scalar(replay("act"))
            block.vector(replay("dve"))
            block.gpsimd(replay("pool"))
            block.sync(replay("sp"))
    return nc


def _colpiece(W, j):
    blk = W[:, 512 * j:512 * (j + 1)]
    return blk.reshape(8, 128, 512).transpose(1, 0, 2).reshape(128, 4096)


def _vec(a):
    return a.reshape(L, 8, 128).transpose(2, 0, 1).reshape(128, L * 8)


def prep_shared(inp):
    f = np.float32
    w_in = np.asarray(inp["w_in"], f)
    w_pa = np.asarray(inp["w_pa"], f)
    w_pb = np.asarray(inp["w_pb"], f)
    w_out = np.asarray(inp["w_out"], f)
    w_ff1 = np.asarray(inp["w_ff1"], f)
    w_ff2 = np.asarray(inp["w_ff2"], f)
    w_ada = np.asarray(inp["w_ada"], f)
    wst = np.empty((L, PIECES_PER_LAYER, 128, 4096), f)
    wada = np.empty((L, ADA_PIECES, 128, 4096), f)
    for l in range(L):
        ps = []
        for j in (4, 6, 5, 7, 2, 3, 0, 1):
            ps.append(_colpiece(w_in[l], j))
        for hf in range(2):
            ps.append(_colpiece(w_pa[l], hf))
            ps.append(_colpiece(w_pb[l], hf))
            ps.append(_colpiece(w_in[l], 8 + hf))
            ps.append(_colpiece(w_in[l], 10 + hf))
        ps.append(_colpiece(w_out[l], 0))
        ps.append(_colpiece(w_out[l], 1))
        for hh in range(2):
            for j in range(4):
                ps.append(_colpiece(w_ff1[l], hh * 4 + j))
            for q in range(4):
                blk = w_ff2[l][hh * 2048:(hh + 1) * 2048, q * 256:(q + 1) * 256]
                ps.append(blk.reshape(16, 128, 256).transpose(1, 0, 2).reshape(128, 4096))
        assert len(ps) == PIECES_PER_LAYER
        for i, p in enumerate(ps):
            wst[l, i] = p
        for j in range(ADA_PIECES):
            wada[l, j] = _colpiece(w_ada[l], j)
    params = np.zeros((128, NPARAM), f)

    def put(name, arr):
        params[:, OFF[name]:OFF[name] + arr.shape[1]] = arr
    put("n1g", _vec(np.asarray(inp["norm1_g"], f)))
    put("n2g", _vec(np.asarray(inp["norm2_g"], f)))
    put("alg", _vec(np.asarray(inp["a_ln_g"], f)))
    put("alb", _vec(np.asarray(inp["a_ln_b"], f)))
    put("cvb", _vec(np.asarray(inp["b_conv_b"], f)))
    put("blg", _vec(np.asarray(inp["b_ln_g"], f)))
    put("blb", _vec(np.asarray(inp["b_ln_b"], f)))
    put("fg", np.asarray(inp["final_g"], f).reshape(8, 128).T)
    put("bada", np.asarray(inp["b_ada"], f).reshape(L, 48, 128).transpose(2, 0, 1).reshape(128, L * 48))
    put("cvw", np.asarray(inp["b_conv_w"], f).reshape(L, CW, 8, 128).transpose(3, 0, 2, 1).reshape(128, L * 8 * CW))
    a_ws = np.asarray(inp["a_ws"], f)
    wsT = np.ascontiguousarray(a_ws.transpose(3, 0, 1, 2).reshape(128, L * 8 * 128))
    mask = np.ascontiguousarray(np.tril(np.ones((128, 128), f)).T)
    bsrow = np.ascontiguousarray(np.asarray(inp["a_bs"], f).reshape(1, L * 8 * 128))
    return {
        "params": params,
        "wada": wada.reshape(L * ADA_PIECES * 128, 4096),
        "wst": wst.reshape(L * PIECES_PER_LAYER * 128, 4096),
        "wsT": wsT, "mask": mask, "bsrow": bsrow, "ident": np.eye(128, dtype=f),
    }


_NC_CACHE = {}


def kernel(**inputs):
    x = np.asarray(inputs["x"], np.float32)
    c = np.asarray(inputs["c"], np.float32)
    B, T, Dm = x.shape
    assert (B, T, Dm) == (16, SEQ, D)
    shared = prep_shared(inputs)
    per = B // NCORES
    in_maps = []
    for i in range(NCORES):
        xs = x[i * per:(i + 1) * per].reshape(per * T, D)
        m = dict(shared)
        m["xT"] = np.ascontiguousarray(xs.T)
        m["cT"] = np.ascontiguousarray(c[i * per:(i + 1) * per].reshape(per, 8, 128).transpose(2, 1, 0).reshape(128, 16))
        in_maps.append(m)
    if "nc" not in _NC_CACHE:
        _NC_CACHE["nc"] = build_nc()
    nc = _NC_CACHE["nc"]
    res = run_bass_kernel_spmd(nc, in_maps, core_ids=list(range(NCORES)))
    out = np.empty((B, T, D), np.float32)
    for i in range(NCORES):
        o = np.asarray(res.results[i]["outT"])
        out[i * per:(i + 1) * per] = o.T.reshape(per, T, D)
    return out
```

```python
import numpy as np
import concourse.bass as bass
import concourse.mybir as mybir
from concourse.bass_utils import run_bass_kernel_spmd

F32 = mybir.dt.float32
BF16 = mybir.dt.bfloat16
AF = mybir.ActivationFunctionType
ALU = mybir.AluOpType

L = 4
D = 1024
KC = 8
SEQ = 4096
NCORES = 8
NT = 1024
ST = 512
NSUB = NT // ST
CW = 31
HALO = CW - 1
EPS = 1e-6
NSLOT = 6
K_DB = 10
K_PE = 26
PIECES_PER_LAYER = 34
ADA_PIECES = 12

OFF = {}
_o = 0
for _name, _n in [("n1g", L * 8), ("n2g", L * 8), ("alg", L * 8), ("alb", L * 8), ("cvb", L * 8),
                  ("blg", L * 8), ("blb", L * 8), ("fg", 8), ("bada", L * 48), ("cvw", L * 8 * CW)]:
    OFF[_name] = _o
    _o += _n
NPARAM = _o


class Buf:
    __slots__ = ("name", "last_w", "reads")

    def __init__(self, name):
        self.name = name
        self.last_w = None
        self.reads = []


class Prog:
    ENGS = ("pe", "act", "dve", "pool", "sp")

    def __init__(self):
        self.q = {e: [] for e in self.ENGS}
        self.count = {}
        self.waited = {e: {} for e in self.ENGS}

    def _deps(self, eng, reads, writes):
        deps = {}

        def add(tok, kind):
            if tok is None:
                return
            key, val = tok
            if key == eng:
                if eng == "pe":
                    return
                if kind != "raw":
                    return
            if deps.get(key, 0) < val:
                deps[key] = val

        for b in reads:
            add(b.last_w, "raw")
        for b in writes:
            add(b.last_w, "waw")
            for t in b.reads:
                add(t, "war")
        out = []
        w = self.waited[eng]
        for key, val in deps.items():
            if w.get(key, 0) < val:
                w[key] = val
                out.append((key, val))
        return out

    def op(self, eng, fn, reads=(), writes=()):
        waits = self._deps(eng, reads, writes)
        self.count[eng] = self.count.get(eng, 0) + 1
        tok = (eng, self.count[eng])
        self.q[eng].append((waits, fn, (eng, 1)))
        for b in reads:
            b.reads.append(tok)
        for b in writes:
            b.last_w = tok
            b.reads = []
        return tok

    def dma(self, eng, semkey, fn, reads=(), writes=()):
        waits = self._deps(eng, reads, writes)
        self.count[semkey] = self.count.get(semkey, 0) + 16
        tok = (semkey, self.count[semkey])
        self.q[eng].append((waits, fn, (semkey, 16)))
        for b in reads:
            b.reads.append(tok)
        for b in writes:
            b.last_w = tok
            b.reads = []
        return tok

    def dma_group(self, eng, semkey, items):
        toks = []
        allw = []
        for fn, reads, writes in items:
            self.dma(eng, semkey, fn, reads=reads, writes=writes)
            allw += list(writes)
        final = (semkey, self.count[semkey])
        for b in allw:
            b.last_w = final
        return final

    def final_wait(self, eng, toks):
        self.q[eng].append(([t for t in toks], None, None))


def bc_mid(ap2d, reps):
    a = ap2d.ap
    return bass.AP(ap2d.tensor, ap2d.offset, [[a[0][0], a[0][1]], [0, reps], [a[-1][0], a[-1][1]]])


def bc_last(ap2d, reps):
    a = ap2d.ap
    return bass.AP(ap2d.tensor, ap2d.offset, [[a[0][0], a[0][1]], [a[-1][0], a[-1][1]], [0, reps]])


def build_nc(n_tiles=8, n_layers=L, tiles_per_seq=4):
    nc = bass.Bass("TRN2", target_bir_lowering=False)
    NTOK = n_tiles * NT
    xT = nc.dram_tensor("xT", [D, NTOK], F32, kind="ExternalInput").ap()
    outT = nc.dram_tensor("outT", [D, NTOK], F32, kind="ExternalOutput").ap()
    params_d = nc.dram_tensor("params", [128, NPARAM], F32, kind="ExternalInput").ap()
    cT_d = nc.dram_tensor("cT", [128, 16], F32, kind="ExternalInput").ap()
    wada_d = nc.dram_tensor("wada", [L * ADA_PIECES * 128, 4096], F32, kind="ExternalInput").ap()
    wst_d = nc.dram_tensor("wst", [L * PIECES_PER_LAYER * 128, 4096], F32, kind="ExternalInput").ap()
    wt_d = nc.dram_tensor("wsT", [128, L * 8 * 128], F32, kind="ExternalInput").ap()
    mask_d = nc.dram_tensor("mask", [128, 128], F32, kind="ExternalInput").ap()
    ident_d = nc.dram_tensor("ident", [128, 128], F32, kind="ExternalInput").ap()
    bs_d = nc.dram_tensor("bsrow", [1, L * 8 * 128], F32, kind="ExternalInput")

    P = Prog()
    n_seq = n_tiles // tiles_per_seq

    from contextlib import ExitStack
    with ExitStack() as es:
        def sb(name, shape, dt):
            return es.enter_context(nc.sbuf_tensor(name, shape, dt))

        def sem(name):
            return es.enter_context(nc.semaphore(name))

        X = sb("X", [128, KC, NT], F32)
        A = sb("A", [128, KC, NT], BF16)
        U = sb("U", [128, 24576], BF16)
        ring = [sb(f"ring{i}", [128, 4096], BF16) for i in range(NSLOT)]
        PRM = sb("PRM", [128, NPARAM], F32)
        MOD = sb("MOD", [128, L * 48 * 2], F32)
        GS = sb("GS", [128, L * 2 * 8 * 2], F32)
        WMT = sb("WMT", [128, L * 8 * 128], BF16)
        CB = sb("CB", [128, 8 * 128], F32)
        MASK = sb("MASK", [128, 128], F32)
        ONES_S = sb("ONES_S", [128, 128], BF16)
        ONES_1 = sb("ONES_1", [128, 128], BF16)
        EPSB = sb("EPSB", [128, 1], F32)
        CIN = sb("CIN", [128, 16], F32)
        CSG = sb("CSG", [128, 16], F32)
        CACT = sb("CACT", [128, 16], BF16)
        ZT = sb("ZT", [128, L * 8 * HALO], BF16)
        ZB = [sb(f"ZB{i}", [128, HALO + ST], BF16) for i in range(3)]
        SQ = [sb(f"SQ{i}", [128, ST], BF16) for i in range(2)]
        FT = [sb(f"FT{i}", [128, ST], F32) for i in range(5)]
        ST1 = [sb(f"ST1_{i}", [128, ST], F32) for i in range(1)]
        RSTD = [sb(f"RSTD{i}", [128, ST], F32) for i in range(2)]
        MU = [sb(f"MU{i}", [128, ST], F32) for i in range(1)]
        VH = [sb(f"VH{i}", [128, 4, D], BF16) for i in range(1)]
        BNS = [sb(f"BNS{i}", [128, 12], F32) for i in range(2)]
        MV = [sb(f"MV{i}", [128, 2], F32) for i in range(2)]
        VT = [sb(f"VT{i}", [128, 2], F32) for i in range(2)]
        DG = sb("DG", [128, (K_PE + K_DB) * 128], BF16)

        banks = [es.enter_context(nc.psum_tensor(f"bank{i}", [128, ST], F32)) for i in range(8)]

        UF = U[:, 0:16384].bitcast(F32)

        def ZCa(cc, s):
            o = (s * KC + cc) * ST
            return UF[:, o:o + ST]

        def Ta(h, s):
            o = (s * KC + h) * ST
            return U[:, o:o + ST]

        def MGa(oc, s):
            o = 8192 + (s * KC + oc) * ST
            return U[:, o:o + ST]
        Qv = U[:, 16384:24576].rearrange("p (k t) -> p k t", k=KC)
        Tv = U[:, 0:8192].rearrange("p (k t) -> p k t", k=KC)
        MGv = U[:, 8192:16384].rearrange("p (k t) -> p k t", k=KC)
        Rv = U[:, 0:16384].rearrange("p (k t) -> p k t", k=16)
        WTF = U[:, 0:8192].bitcast(F32)

        sems = {e: sem(f"s_{e}") for e in ("pe", "act", "dve", "pool")}
        for i in range(NSLOT):
            sems[f"ring{i}"] = sem(f"s_ring{i}")
        for k in ("ldx", "ldp", "st0", "st1", "st2", "st3", "st4", "bsd", "ldi"):
            sems[k] = sem(f"s_{k}")

        bX = [[Buf(f"X{k}_{s}") for s in range(NSUB)] for k in range(KC)]
        bA = [[Buf(f"A{k}_{s}") for s in range(NSUB)] for k in range(KC)]
        bUg = [Buf(f"U{g}") for g in range(48)]
        bU_ZC = [bUg[cc * 4:cc * 4 + 4] for cc in range(KC)]
        bU_ZCs = [[bUg[2 * (s * KC + cc):2 * (s * KC + cc) + 2] for s in range(NSUB)] for cc in range(KC)]
        bQ = [[[bUg[32 + k * 2 + s]] for s in range(NSUB)] for k in range(KC)]
        bT = [[[bUg[s * KC + k]] for s in range(NSUB)] for k in range(KC)]
        bMG = [[[bUg[16 + s * KC + k]] for s in range(NSUB)] for k in range(KC)]
        bR = [[[bUg[k * 2 + s]] for s in range(NSUB)] for k in range(16)]
        bring = [Buf(f"ring{i}") for i in range(NSLOT)]
        bbank = [Buf(f"bank{i}") for i in range(8)]
        bPRM = Buf("PRM"); bMOD = Buf("MOD"); bGS = Buf("GS"); bWMT = Buf("WMT"); bCB = Buf("CB")
        bMASK = Buf("MASK"); bCONST = Buf("CONST"); bCIN = Buf("CIN"); bCSG = Buf("CSG"); bCACT = Buf("CACT")
        bZT = [[Buf(f"ZT{l}_{k}") for k in range(KC)] for l in range(L)]
        bZB = [Buf(f"ZB{i}") for i in range(3)]
        bSQ = [Buf(f"SQ{i}") for i in range(2)]
        bFT = [Buf(f"FT{i}") for i in range(5)]
        bST1 = [Buf(f"ST1{i}") for i in range(1)]
        bRSTD = [Buf(f"RSTD{i}") for i in range(2)]
        bMU = [Buf(f"MU{i}") for i in range(1)]
        bVH = [[Buf(f"VH{i}_{c}") for c in range(4)] for i in range(1)]
        bBNS = [Buf(f"BNS{i}") for i in range(2)]
        bMV = [Buf(f"MV{i}") for i in range(2)]
        bVT = [Buf(f"VT{i}") for i in range(2)]
        bDG = [Buf(f"DG{k}") for k in range(K_PE + K_DB)]
        bWTF = bUg[0:16]

        rot = {}

        def nxt(name, n):
            i = rot.get(name, 0)
            rot[name] = (i + 1) % n
            return i

        piece_src = []
        for l in range(L):
            for j in range(ADA_PIECES):
                r0 = (l * ADA_PIECES + j) * 128
                piece_src.append(wada_d[r0:r0 + 128, :])
        n_ada = len(piece_src)
        for t in range(n_tiles):
            for l in range(n_layers):
                for j in range(PIECES_PER_LAYER):
                    r0 = (l * PIECES_PER_LAYER + j) * 128
                    piece_src.append(wst_d[r0:r0 + 128, :])
        issued = [0]

        def issue_piece(n):
            assert n == issued[0]
            if n >= len(piece_src):
                return
            slot = n % NSLOT
            src = piece_src[n]
            P.dma("pool", f"ring{slot}",
                  lambda e, slot=slot, src=src: e.dma_start(out=ring[slot][:], in_=src),
                  writes=[bring[slot]])
            issued[0] += 1

        def release_piece(n):
            issue_piece_target = n + NSLOT
            while issued[0] <= issue_piece_target and issued[0] < len(piece_src):
                issue_piece(issued[0])

        for n in range(NSLOT):
            issue_piece(n)

        def slot_of(n):
            assert n < issued[0], (n, issued[0])
            return ring[n % NSLOT], bring[n % NSLOT]

        def mm_group(bank_i, pairs, reads, out_ap=None, writes=None):
            o = out_ap if out_ap is not None else banks[bank_i][:]
            n = len(pairs)

            def fn(e, o=o, pairs=pairs, n=n):
                ins = None
                for i, (lt, rh) in enumerate(pairs):
                    ins = e.matmul(o, lt, rh, start=(i == 0), stop=(i == n - 1))
                return ins
            return P.op("pe", fn, reads=reads, writes=writes if writes is not None else [bbank[bank_i]])

        def prm(name, idx):
            o = OFF[name] + idx
            return PRM[:, o:o + 1]

        def gs_ap(l, which, kc, s):
            o = ((l * 2 + which) * 8 + kc) * 2 + s
            return GS[:, o:o + 1]

        def mod_ap(l, m, kc, s):
            o = (l * 48 + m * 8 + kc) * 2 + s
            return MOD[:, o:o + 1]

        P.dma_group("sp", "ldp", [
            (lambda e: e.dma_start(out=PRM[:], in_=params_d[:, :]), [], [bPRM]),
            (lambda e: e.dma_start(out=CIN[:], in_=cT_d[:, :]), [], [bCIN]),
            (lambda e: e.dma_start(out=MASK[:], in_=mask_d[:, :]), [], [bMASK]),
            (lambda e: e.dma_start(out=WTF, in_=wt_d[:, :]), [], bWTF),
        ])
        P.op("dve", lambda e: e.memset(ONES_S[:], 1.0 / 1024.0), writes=[bCONST])
        P.op("dve", lambda e: e.memset(ONES_1[:], 1.0), writes=[bCONST])
        P.op("dve", lambda e: e.memset(EPSB[:], EPS), writes=[bCONST])
        P.op("act", lambda e: e.activation(out=CSG[:], in_=CIN[:], func=AF.Sigmoid), reads=[bCIN], writes=[bCSG])
        P.op("dve", lambda e: e.tensor_tensor(out=CACT[:], in0=CIN[:], in1=CSG[:], op=ALU.mult),
             reads=[bCIN, bCSG], writes=[bCACT])
        pn = 0
        for l in range(L):
            for j in range(ADA_PIECES):
                slot, bslot = slot_of(pn)
                bk = nxt("bank", 6)
                first = True
                for ocl in range(4):
                    pairs = [(slot[:, kc * 512 + ocl * 128: kc * 512 + (ocl + 1) * 128], CACT[:, kc * 2:(kc + 1) * 2])
                             for kc in range(KC)]
                    mm_group(bk, pairs, reads=[bslot, bCACT], out_ap=banks[bk][:, ocl * 2:(ocl + 1) * 2],
                             writes=[bbank[bk]] if first else [])
                    if not first:
                        bbank[bk].last_w = ("pe", P.count["pe"])
                    first = False
                o = (l * 48 + 4 * j) * 2
                bo = OFF["bada"] + l * 48 + 4 * j
                P.op("dve", lambda e, bk=bk, o=o, bo=bo: e.tensor_tensor(
                    out=MOD[:, o:o + 8].rearrange("p (c s) -> p c s", s=2),
                    in0=banks[bk][:, 0:8].rearrange("p (c s) -> p c s", s=2),
                    in1=bc_last(PRM[:, bo:bo + 4], 2), op=ALU.add),
                    reads=[bbank[bk], bPRM], writes=[bMOD])
                release_piece(pn)
                pn += 1
        for l in range(L):
            for which, m, gname in ((0, 1, "n1g"), (1, 4, "n2g")):
                o = (l * 2 + which) * 16
                mo = (l * 48 + m * 8) * 2
                go = OFF[gname] + l * 8
                P.op("dve", lambda e, o=o, mo=mo: e.tensor_scalar(
                    out=GS[:, o:o + 16], in0=MOD[:, mo:mo + 16], scalar1=1.0, scalar2=None, op0=ALU.add),
                    reads=[bMOD], writes=[bGS])
                P.op("dve", lambda e, o=o, go=go: e.tensor_tensor(
                    out=GS[:, o:o + 16].rearrange("p (c s) -> p c s", s=2),
                    in0=GS[:, o:o + 16].rearrange("p (c s) -> p c s", s=2),
                    in1=bc_last(PRM[:, go:go + 8], 2), op=ALU.mult),
                    reads=[bGS, bPRM], writes=[bGS])
        P.op("dve", lambda e: e.tensor_tensor(
            out=WMT[:].rearrange("p (g i) -> p g i", i=128),
            in0=WTF.rearrange("p (g i) -> p g i", i=128),
            in1=bc_mid(MASK[:], L * 8), op=ALU.mult),
            reads=bWTF + [bMASK], writes=[bWMT])
        P.dma("sp", "ldi", lambda e: e.dma_start(out=MASK[:], in_=ident_d[:, :]), writes=[bMASK])
        def gate_prep(l):
            P.dma("sp", "bsd", lambda e: e.dma_start(
                out=CB[:], in_=bass.AP(bs_d, l * 1024, [[0, 128], [1, 1024]])), writes=[bCB])
            for hf in range(2):
                bk = nxt("bank", 6)
                c0 = l * 1024 + hf * 512
                mm_group(bk, [(ONES_1[:], WMT[:, c0:c0 + 512])], reads=[bWMT, bCONST])
                for hh in range(4):
                    h = hf * 4 + hh
                    cc0 = h * 128
                    bidx = OFF["alb"] + l * 8 + h
                    P.op("dve", lambda e, bk=bk, hh=hh, cc0=cc0, bidx=bidx: e.scalar_tensor_tensor(
                        out=CB[:, cc0:cc0 + 128], in0=banks[bk][:, hh * 128:(hh + 1) * 128],
                        scalar=PRM[:, bidx:bidx + 1], in1=CB[:, cc0:cc0 + 128], op0=ALU.mult, op1=ALU.add),
                        reads=[bbank[bk], bPRM, bCB], writes=[bCB])

        def rms_stats(src_bufs_fn, src_ap_fn, s):
            bk = nxt("bank", 6)
            for kc in range(KC):
                qi = nxt("SQ", 2)
                P.op("act", lambda e, kc=kc, qi=qi: e.activation(out=SQ[qi][:], in_=src_ap_fn(kc, s), func=AF.Square),
                     reads=[src_bufs_fn(kc, s)], writes=[bSQ[qi]])

                def fn(e, kc=kc, qi=qi, bk=bk):
                    return e.matmul(banks[bk][:], ONES_S[:], SQ[qi][:], start=(kc == 0), stop=(kc == KC - 1))
                P.op("pe", fn, reads=[bSQ[qi], bCONST], writes=[bbank[bk]] if kc == 0 else [])
            bbank[bk].last_w = ("pe", P.count["pe"])
            si = nxt("ST1", 1)
            P.op("act", lambda e, bk=bk, si=si: e.activation(
                out=ST1[si][:], in_=banks[bk][:], func=AF.Sqrt, bias=EPSB[:, 0:1]),
                reads=[bbank[bk], bCONST], writes=[bST1[si]])
            ri = nxt("RSTD", 2)
            P.op("dve", lambda e, si=si, ri=ri: e.reciprocal(out=RSTD[ri][:], in_=ST1[si][:]),
                 reads=[bST1[si]], writes=[bRSTD[ri]])
            return ri

        def norm_gen(l, which, sq, s):
            shm = 0 if which == 0 else 3
            ri = rms_stats(lambda kc, s: bX[kc][s], lambda kc, s: X[:, kc, s * ST:(s + 1) * ST], s)
            yield
            for kc in range(KC):
                fi = nxt("FT", 5)
                P.op("dve", lambda e, kc=kc, s=s, fi=fi, ri=ri: e.tensor_tensor(
                    out=FT[fi][:], in0=X[:, kc, s * ST:(s + 1) * ST], in1=RSTD[ri][:], op=ALU.mult),
                    reads=[bX[kc][s], bRSTD[ri]], writes=[bFT[fi]])
                P.op("act", lambda e, kc=kc, s=s, fi=fi: e.activation(
                    out=A[:, kc, s * ST:(s + 1) * ST], in_=FT[fi][:], func=AF.Identity,
                    scale=gs_ap(l, which, kc, sq), bias=mod_ap(l, shm, kc, sq)),
                    reads=[bFT[fi], bGS, bMOD], writes=[bA[kc][s]])
                if kc % 2 == 1:
                    yield

        def run(g):
            for _ in g:
                pass

        def interleave(*gens):
            gens = list(gens)
            while gens:
                for g in list(gens):
                    try:
                        next(g)
                    except StopIteration:
                        gens.remove(g)

        def A_all(s):
            return [bA[kc][s] for kc in range(KC)]

        def conv_pe_gen(l, pn0, s, zero_halo):
            zis = {}
            tsl = slice(s * ST, (s + 1) * ST)

            def ag(cc):
                half, cl = divmod(cc, 4)
                sa, bsa = slot_of(pn0 + 2 * half)
                sg, bsg = slot_of(pn0 + 2 * half + 1)
                zi = nxt("ZB", 3)
                zis[cc] = zi
                zo = (l * 8 + cc) * HALO
                if zero_halo:
                    P.op("act", lambda e, zi=zi: e.memzero(ZB[zi][:, 0:HALO]), writes=[bZB[zi]])
                else:
                    P.op("act", lambda e, zi=zi, zo=zo: e.copy(out=ZB[zi][:, 0:HALO], in_=ZT[:, zo:zo + HALO]),
                         reads=[bZT[l][cc]], writes=[bZB[zi]])
                ba = nxt("bank", 6)
                mm_group(ba, [(sa[:, kc * 512 + cl * 128: kc * 512 + (cl + 1) * 128], A[:, kc, tsl])
                              for kc in range(KC)], reads=[bsa] + A_all(s))
                bg = nxt("bank", 6)
                mm_group(bg, [(sg[:, kc * 512 + cl * 128: kc * 512 + (cl + 1) * 128], A[:, kc, tsl])
                              for kc in range(KC)], reads=[bsg] + A_all(s))
                fi = nxt("FT", 5)
                P.op("act", lambda e, bg=bg, fi=fi: e.activation(out=FT[fi][:], in_=banks[bg][:], func=AF.Sigmoid),
                     reads=[bbank[bg]], writes=[bFT[fi]])
                P.op("dve", lambda e, ba=ba, fi=fi, zi=zi: e.tensor_tensor(
                    out=ZB[zi][:, HALO:HALO + ST], in0=banks[ba][:], in1=FT[fi][:], op=ALU.mult),
                    reads=[bbank[ba], bFT[fi]], writes=[bZB[zi]])

            dgs = {}

            def dgi(par, k):
                return par * K_DB + k if k < K_DB else 2 * K_DB + (k - K_DB)

            def diag(cc, lo, hi):
                if lo == 0:
                    dgs[cc] = nxt("DGP", 2)
                par = dgs[cc]
                wo = OFF["cvw"] + (l * 8 + cc) * CW
                for k in range(lo, hi):
                    d0 = dgi(par, k) * 128
                    P.op("act", lambda e, k=k, wo=wo, d0=d0: e.activation(
                        out=DG[:, d0:d0 + 128], in_=MASK[:], func=AF.Identity, scale=PRM[:, wo + k:wo + k + 1]),
                        reads=[bMASK, bPRM], writes=[bDG[dgi(par, k)]])
                if lo == 0:
                    zi = zis[cc]
                    zo = (l * 8 + cc) * HALO
                    P.op("act", lambda e, zi=zi, zo=zo: e.copy(out=ZT[:, zo:zo + HALO], in_=ZB[zi][:, ST:ST + HALO]),
                         reads=[bZB[zi]], writes=[bZT[l][cc]])

            def taps(cc):
                zi = zis[cc]
                par = dgs[cc]
                wo = OFF["cvw"] + (l * 8 + cc) * CW
                bo = OFF["cvb"] + l * 8 + cc
                bk = nxt("bank", 6)
                for k in range(K_PE):
                    d0 = dgi(par, k) * 128
                    P.op("pe", lambda e, k=k, zi=zi, bk=bk, d0=d0: e.matmul(
                        banks[bk][:], DG[:, d0:d0 + 128], ZB[zi][:, k:k + ST],
                        start=(k == 0), stop=(k == K_PE - 1)),
                        reads=[bDG[dgi(par, k)], bZB[zi]], writes=[bbank[bk]] if k == 0 else [])
                bbank[bk].last_w = ("pe", P.count["pe"])
                P.op("dve", lambda e, cc=cc, bk=bk, bo=bo: e.tensor_scalar(
                    out=ZCa(cc, s), in0=banks[bk][:], scalar1=PRM[:, bo:bo + 1], scalar2=None,
                    op0=ALU.add), reads=[bbank[bk], bPRM], writes=bU_ZCs[cc][s])
                for k in range(K_PE, CW):
                    P.op("dve", lambda e, cc=cc, zi=zi, wo=wo, k=k: e.scalar_tensor_tensor(
                        out=ZCa(cc, s), in0=ZB[zi][:, k:k + ST], scalar=PRM[:, wo + k:wo + k + 1],
                        in1=ZCa(cc, s), op0=ALU.mult, op1=ALU.add),
                        reads=[bZB[zi], bPRM] + bU_ZCs[cc][s], writes=bU_ZCs[cc][s])

            ag(0)
            diag(0, 0, K_DB)
            diag(0, K_DB, K_PE)
            for cc in range(KC):
                if cc + 1 < KC:
                    ag(cc + 1)
                    diag(cc + 1, 0, K_DB)
                taps(cc)
                if cc + 1 < KC:
                    diag(cc + 1, K_DB, K_PE)
                yield

        def ln_gen(l, s):
            bm, bx = 6, 7
            for cc in range(KC):
                q1 = nxt("SQ", 2)
                P.op("act", lambda e, cc=cc, q1=q1: e.copy(out=SQ[q1][:], in_=ZCa(cc, s)),
                     reads=bU_ZCs[cc][s], writes=[bSQ[q1]])
                P.op("pe", lambda e, cc=cc, q1=q1, bm=bm: e.matmul(
                    banks[bm][:], ONES_S[:], SQ[q1][:], start=(cc == 0), stop=(cc == KC - 1)),
                    reads=[bSQ[q1], bCONST], writes=[bbank[bm]] if cc == 0 else [])
                q2 = nxt("SQ", 2)
                P.op("act", lambda e, cc=cc, q2=q2: e.activation(
                    out=SQ[q2][:], in_=ZCa(cc, s), func=AF.Square),
                    reads=bU_ZCs[cc][s], writes=[bSQ[q2]])
                P.op("pe", lambda e, cc=cc, q2=q2, bx=bx: e.matmul(
                    banks[bx][:], ONES_S[:], SQ[q2][:], start=(cc == 0), stop=(cc == KC - 1)),
                    reads=[bSQ[q2], bCONST], writes=[bbank[bx]] if cc == 0 else [])
                if cc % 2 == 1:
                    yield
            bbank[bm].last_w = ("pe", P.count["pe"])
            bbank[bx].last_w = ("pe", P.count["pe"])
            mi = nxt("MU", 1)
            P.op("dve", lambda e, bm=bm, mi=mi: e.tensor_copy(out=MU[mi][:], in_=banks[bm][:]),
                 reads=[bbank[bm]], writes=[bMU[mi]])
            si = nxt("ST1", 1)
            P.op("dve", lambda e, mi=mi, si=si: e.tensor_tensor(out=ST1[si][:], in0=MU[mi][:], in1=MU[mi][:], op=ALU.mult),
                 reads=[bMU[mi]], writes=[bST1[si]])
            P.op("dve", lambda e, bx=bx, si=si: e.scalar_tensor_tensor(
                out=ST1[si][:], in0=banks[bx][:], scalar=EPS, in1=ST1[si][:], op0=ALU.add, op1=ALU.subtract),
                reads=[bbank[bx], bST1[si]], writes=[bST1[si]])
            P.op("act", lambda e, si=si: e.activation(out=ST1[si][:], in_=ST1[si][:], func=AF.Sqrt),
                 reads=[bST1[si]], writes=[bST1[si]])
            ri = nxt("RSTD", 2)
            P.op("dve", lambda e, si=si, ri=ri: e.reciprocal(out=RSTD[ri][:], in_=ST1[si][:]),
                 reads=[bST1[si]], writes=[bRSTD[ri]])
            yield
            for cc in range(KC):
                f1 = nxt("FT", 5)
                P.op("dve", lambda e, cc=cc, f1=f1, mi=mi: e.tensor_tensor(
                    out=FT[f1][:], in0=ZCa(cc, s), in1=MU[mi][:], op=ALU.subtract),
                    reads=bU_ZCs[cc][s] + [bMU[mi]], writes=[bFT[f1]])
                f2 = nxt("FT", 5)
                P.op("dve", lambda e, f1=f1, f2=f2, ri=ri: e.tensor_tensor(
                    out=FT[f2][:], in0=FT[f1][:], in1=RSTD[ri][:], op=ALU.mult),
                    reads=[bFT[f1], bRSTD[ri]], writes=[bFT[f2]])
                gi = OFF["blg"] + l * 8 + cc
                bi = OFF["blb"] + l * 8 + cc
                f3 = nxt("FT", 5)
                P.op("act", lambda e, f2=f2, f3=f3, gi=gi, bi=bi: e.activation(
                    out=FT[f3][:], in_=FT[f2][:], func=AF.Sigmoid, scale=PRM[:, gi:gi + 1], bias=PRM[:, bi:bi + 1]),
                    reads=[bFT[f2], bPRM], writes=[bFT[f3]])
                P.op("act", lambda e, f2=f2, f1=f1, gi=gi, bi=bi: e.activation(
                    out=FT[f1][:], in_=FT[f2][:], func=AF.Identity, scale=PRM[:, gi:gi + 1], bias=PRM[:, bi:bi + 1]),
                    reads=[bFT[f2], bPRM], writes=[bFT[f1]])
                P.op("dve", lambda e, cc=cc, f1=f1, f3=f3: e.tensor_tensor(
                    out=Qv[:, cc, s * ST:(s + 1) * ST], in0=FT[f1][:], in1=FT[f3][:], op=ALU.mult),
                    reads=[bFT[f1], bFT[f3]], writes=bQ[cc][s])
                yield

        def gate_gen(l, pn0, s):
            sv = [slot_of(pn0), slot_of(pn0 + 1)]
            su = [slot_of(pn0 + 2), slot_of(pn0 + 3)]
            if True:
                for c in range(4):
                    tok0 = s * ST + c * 128
                    pv = []
                    for hv in range(2):
                        bk = nxt("bank", 6)
                        mm_group(bk, [(A[:, kc, tok0:tok0 + 128], sv[hv][0][:, kc * 512:(kc + 1) * 512]) for kc in range(KC)],
                                 reads=[sv[hv][1]] + A_all(s))
                        pv.append(bk)
                    bi = nxt("BNS", 2)
                    P.op("dve", lambda e, bi=bi, b0=pv[0]: e.bn_stats(out=BNS[bi][:, 0:6], in_=banks[b0][:]),
                         reads=[bbank[pv[0]]], writes=[bBNS[bi]])
                    P.op("dve", lambda e, bi=bi, b1=pv[1]: e.bn_stats(out=BNS[bi][:, 6:12], in_=banks[b1][:]),
                         reads=[bbank[pv[1]], bBNS[bi]], writes=[bBNS[bi]])
                    P.op("dve", lambda e, bi=bi: e.bn_aggr(out=MV[bi][:], in_=BNS[bi][:]),
                         reads=[bBNS[bi]], writes=[bMV[bi]])
                    P.op("act", lambda e, bi=bi: e.activation(
                        out=VT[bi][:, 0:1], in_=MV[bi][:, 1:2], func=AF.Sqrt, bias=EPSB[:, 0:1]),
                        reads=[bMV[bi], bCONST], writes=[bVT[bi]])
                    P.op("dve", lambda e, bi=bi: e.reciprocal(out=VT[bi][:, 1:2], in_=VT[bi][:, 0:1]),
                         reads=[bVT[bi]], writes=[bVT[bi]])
                    for hv in range(2):
                        P.op("dve", lambda e, bi=bi, c=c, hv=hv, bk=pv[hv]: e.tensor_scalar(
                            out=VH[0][:, c, hv * 512:(hv + 1) * 512], in0=banks[bk][:],
                            scalar1=MV[bi][:, 0:1], scalar2=VT[bi][:, 1:2], op0=ALU.subtract, op1=ALU.mult),
                            reads=[bbank[pv[hv]], bMV[bi], bVT[bi]], writes=[bVH[0][c]])
                    yield
                for h in range(KC):
                    usl, busl = su[h // 4]
                    hl = h % 4
                    bu = nxt("bank", 6)
                    mm_group(bu, [(usl[:, kc * 512 + hl * 128: kc * 512 + (hl + 1) * 128], A[:, kc, s * ST:(s + 1) * ST])
                                  for kc in range(KC)], reads=[busl] + A_all(s))
                    bs0 = nxt("bank", 6)
                    wo = (l * 8 + h) * 128
                    first = True
                    for c in range(4):
                        mm_group(bs0, [(VH[0][:, c, h * 128:(h + 1) * 128], WMT[:, wo:wo + 128])],
                                 reads=[bVH[0][c], bWMT], out_ap=banks[bs0][:, c * 128:(c + 1) * 128],
                                 writes=[bbank[bs0]] if first else [])
                        first = False
                    bbank[bs0].last_w = ("pe", P.count["pe"])
                    fi = nxt("FT", 5)
                    gi = OFF["alg"] + l * 8 + h
                    P.op("dve", lambda e, bs0=bs0, fi=fi, gi=gi, h=h: e.scalar_tensor_tensor(
                        out=FT[fi][:].rearrange("p (c i) -> p c i", c=4),
                        in0=banks[bs0][:].rearrange("p (c i) -> p c i", c=4),
                        scalar=PRM[:, gi:gi + 1], in1=bc_mid(CB[:, h * 128:(h + 1) * 128], 4), op0=ALU.mult, op1=ALU.add),
                        reads=[bbank[bs0], bPRM, bCB], writes=[bFT[fi]])
                    P.op("dve", lambda e, bu=bu, fi=fi, h=h, s=s: e.tensor_tensor(
                        out=Ta(h, s), in0=banks[bu][:], in1=FT[fi][:], op=ALU.mult),
                        reads=[bbank[bu], bFT[fi]], writes=bT[h][s])
                    yield

        def merge_stage(l, pn0):
            for hf in range(2):
                spa, bpa = slot_of(pn0 + 4 * hf)
                spb, bpb = slot_of(pn0 + 4 * hf + 1)
                sga, bga = slot_of(pn0 + 4 * hf + 2)
                sgb, bgb = slot_of(pn0 + 4 * hf + 3)
                for s in range(NSUB):
                    for ocl in range(4):
                        oc = hf * 4 + ocl

                        def wcol(sl, kc, ocl=ocl):
                            return sl[:, kc * 512 + ocl * 128: kc * 512 + (ocl + 1) * 128]
                        tsl = slice(s * ST, (s + 1) * ST)
                        bga_k = nxt("bank", 6)
                        mm_group(bga_k, [(wcol(sga, kc), A[:, kc, tsl]) for kc in range(KC)], reads=[bga] + A_all(s))
                        bgb_k = nxt("bank", 6)
                        mm_group(bgb_k, [(wcol(sgb, kc), A[:, kc, tsl]) for kc in range(KC)], reads=[bgb] + A_all(s))
                        bya = nxt("bank", 6)
                        mm_group(bya, [(wcol(spa, kc), Ta(kc, s)) for kc in range(KC)],
                                 reads=[bpa] + [bT[kc][s][0] for kc in range(KC)])
                        byb = nxt("bank", 6)
                        mm_group(byb, [(wcol(spb, kc), Qv[:, kc, tsl]) for kc in range(KC)],
                                 reads=[bpb] + [bQ[kc][s][0] for kc in range(KC)])
                        f1 = nxt("FT", 5)
                        P.op("act", lambda e, b=bga_k, f1=f1: e.activation(out=FT[f1][:], in_=banks[b][:], func=AF.Sigmoid),
                             reads=[bbank[bga_k]], writes=[bFT[f1]])
                        f2 = nxt("FT", 5)
                        P.op("act", lambda e, b=bgb_k, f2=f2: e.activation(out=FT[f2][:], in_=banks[b][:], func=AF.Sigmoid),
                             reads=[bbank[bgb_k]], writes=[bFT[f2]])
                        P.op("dve", lambda e, b=bya, f1=f1: e.tensor_tensor(out=FT[f1][:], in0=banks[b][:], in1=FT[f1][:], op=ALU.mult),
                             reads=[bbank[bya], bFT[f1]], writes=[bFT[f1]])
                        P.op("dve", lambda e, b=byb, f2=f2: e.tensor_tensor(out=FT[f2][:], in0=banks[b][:], in1=FT[f2][:], op=ALU.mult),
                             reads=[bbank[byb], bFT[f2]], writes=[bFT[f2]])
                        P.op("dve", lambda e, f1=f1, f2=f2, oc=oc, s=s: e.tensor_tensor(
                            out=MGa(oc, s), in0=FT[f1][:], in1=FT[f2][:], op=ALU.add),
                            reads=[bFT[f1], bFT[f2]], writes=bMG[oc][s])
                for i in range(4):
                    release_piece(pn0 + 4 * hf + i)

        def out_gen(l, pn0, sq, s):
            tsl = slice(s * ST, (s + 1) * ST)
            for hf in range(2):
                so, bso = slot_of(pn0 + hf)
                for ocl in range(4):
                    oc = hf * 4 + ocl
                    bk = nxt("bank", 6)
                    mm_group(bk, [(so[:, kc * 512 + ocl * 128: kc * 512 + (ocl + 1) * 128], MGa(kc, s)) for kc in range(KC)],
                             reads=[bso] + [bMG[kc][s][0] for kc in range(KC)])
                    P.op("dve", lambda e, bk=bk, oc=oc: e.scalar_tensor_tensor(
                        out=X[:, oc, tsl], in0=banks[bk][:], scalar=mod_ap(l, 2, oc, sq), in1=X[:, oc, tsl],
                        op0=ALU.mult, op1=ALU.add),
                        reads=[bbank[bk], bMOD, bX[oc][s]], writes=[bX[oc][s]])
                    if ocl % 2 == 1:
                        yield

        def ff1_gen(p1, s_list):
            for s in s_list:
                tsl = slice(s * ST, (s + 1) * ST)
                for j in range(4):
                    sf, bsf = slot_of(p1 + j)
                    for hcl in range(4):
                        hc = j * 4 + hcl
                        bk = nxt("bank", 6)
                        mm_group(bk, [(sf[:, kc * 512 + hcl * 128: kc * 512 + (hcl + 1) * 128], A[:, kc, tsl]) for kc in range(KC)],
                                 reads=[bsf] + A_all(s))
                        fi = nxt("FT", 5)
                        P.op("act", lambda e, bk=bk, fi=fi: e.activation(out=FT[fi][:], in_=banks[bk][:], func=AF.Relu),
                             reads=[bbank[bk]], writes=[bFT[fi]])
                        P.op("act", lambda e, fi=fi, hc=hc, tsl=tsl: e.activation(
                            out=Rv[:, hc, tsl], in_=FT[fi][:], func=AF.Square),
                            reads=[bFT[fi]], writes=bR[hc][s])
                        if hcl % 2 == 1:
                            yield

        def ff2_gen(l, p2, sq, s_list):
            for s in s_list:
                tsl = slice(s * ST, (s + 1) * ST)
                for q in range(4):
                    s2, bs2 = slot_of(p2 + q)
                    for ocl in range(2):
                        oc = q * 2 + ocl
                        bk = nxt("bank", 6)
                        mm_group(bk, [(s2[:, kc * 256 + ocl * 128: kc * 256 + (ocl + 1) * 128], Rv[:, kc, tsl]) for kc in range(16)],
                                 reads=[bs2] + [bR[kc][s][0] for kc in range(16)])
                        P.op("dve", lambda e, bk=bk, oc=oc, tsl=tsl: e.scalar_tensor_tensor(
                            out=X[:, oc, tsl], in0=banks[bk][:], scalar=mod_ap(l, 5, oc, sq), in1=X[:, oc, tsl],
                            op0=ALU.mult, op1=ALU.add),
                            reads=[bbank[bk], bMOD, bX[oc][s]], writes=[bX[oc][s]])
                        yield

        def final_stage(t):
            for s in range(NSUB):
                ri = rms_stats(lambda kc, s: bX[kc][s], lambda kc, s: X[:, kc, s * ST:(s + 1) * ST], s)
                for kc in range(KC):
                    fi = nxt("FT", 5)
                    P.op("dve", lambda e, kc=kc, s=s, fi=fi, ri=ri: e.tensor_tensor(
                        out=FT[fi][:], in0=X[:, kc, s * ST:(s + 1) * ST], in1=RSTD[ri][:], op=ALU.mult),
                        reads=[bX[kc][s], bRSTD[ri]], writes=[bFT[fi]])
                    oi = nxt("FT", 5)
                    fo = OFF["fg"] + kc
                    P.op("act", lambda e, fi=fi, oi=oi, fo=fo: e.activation(
                        out=FT[oi][:], in_=FT[fi][:], func=AF.Identity, scale=PRM[:, fo:fo + 1]),
                        reads=[bFT[fi], bPRM], writes=[bFT[oi]])
                    c0 = t * NT + s * ST
                    P.dma("sp", f"st{oi}", lambda e, oi=oi, kc=kc, c0=c0: e.dma_start(
                        out=outT[kc * 128:(kc + 1) * 128, c0:c0 + ST], in_=FT[oi][:]),
                        reads=[bFT[oi]])

        pn = n_ada
        for t in range(n_tiles):
            sq = t // tiles_per_seq
            first_in_seq = (t % tiles_per_seq == 0)
            P.dma_group("sp", "ldx", [
                (lambda e, kc=kc, t=t: e.dma_start(out=X[:, kc, :], in_=xT[kc * 128:(kc + 1) * 128, t * NT:(t + 1) * NT]),
                 [], [bX[kc][0], bX[kc][1]]) for kc in range(KC)])
            norm0_done = False
            for l in range(n_layers):
                gate_prep(l)
                if not norm0_done:
                    run(norm_gen(l, 0, sq, 0))
                interleave(norm_gen(l, 0, sq, 1), conv_pe_gen(l, pn, 0, first_in_seq))
                interleave(conv_pe_gen(l, pn, 1, False), ln_gen(l, 0))
                for i in range(4):
                    release_piece(pn + i)
                run(gate_gen(l, pn + 4, 0))
                interleave(ln_gen(l, 1), gate_gen(l, pn + 4, 1))
                for i in range(4):
                    release_piece(pn + 4 + i)
                merge_stage(l, pn + 8)
                run(out_gen(l, pn + 16, sq, 0))
                interleave(out_gen(l, pn + 16, sq, 1), norm_gen(l, 1, sq, 0))
                release_piece(pn + 16)
                release_piece(pn + 17)
                interleave(norm_gen(l, 1, sq, 1), ff1_gen(pn + 18, [0]))
                run(ff1_gen(pn + 18, [1]))
                for i in range(4):
                    release_piece(pn + 18 + i)
                run(ff2_gen(l, pn + 22, sq, [0, 1]))
                for i in range(4):
                    release_piece(pn + 22 + i)
                run(ff1_gen(pn + 26, [0, 1]))
                for i in range(4):
                    release_piece(pn + 26 + i)
                run(ff2_gen(l, pn + 30, sq, [0]))
                if l + 1 < n_layers:
                    interleave(ff2_gen(l, pn + 30, sq, [1]), norm_gen(l + 1, 0, sq, 0))
                    norm0_done = True
                else:
                    run(ff2_gen(l, pn + 30, sq, [1]))
                    norm0_done = False
                for i in range(4):
                    release_piece(pn + 30 + i)
                pn += PIECES_PER_LAYER
            final_stage(t)

        P.final_wait("sp", [(f"st{i}", P.count.get(f"st{i}", 0)) for i in range(5) if P.count.get(f"st{i}", 0)])

        with nc.Block() as block:
            def replay(engname):
                def run(e):
                    for waits, fn, sig in P.q[engname]:
                        for key, val in waits:
                            e.wait_ge(sems[key], val)
                        if fn is None:
                            continue
                        ins = fn(e)
                        ins.then_inc(sems[sig[0]], sig[1])
                return run

            block.tensor(replay("pe"))
            block.scalar(replay("act"))
            block.vector(replay("dve"))
            block.gpsimd(replay("pool"))
            block.sync(replay("sp"))
    return nc


def _colpiece(W, j):
    blk = W[:, 512 * j:512 * (j + 1)]
    return blk.reshape(8, 128, 512).transpose(1, 0, 2).reshape(128, 4096)


def _vec(a):
    return a.reshape(L, 8, 128).transpose(2, 0, 1).reshape(128, L * 8)


def prep_shared(inp):
    f = np.float32
    w_in = np.asarray(inp["w_in"], f)
    w_pa = np.asarray(inp["w_pa"], f)
    w_pb = np.asarray(inp["w_pb"], f)
    w_out = np.asarray(inp["w_out"], f)
    w_ff1 = np.asarray(inp["w_ff1"], f)
    w_ff2 = np.asarray(inp["w_ff2"], f)
    w_ada = np.asarray(inp["w_ada"], f)
    wst = np.empty((L, PIECES_PER_LAYER, 128, 4096), f)
    wada = np.empty((L, ADA_PIECES, 128, 4096), f)
    for l in range(L):
        ps = []
        for j in (4, 6, 5, 7, 2, 3, 0, 1):
            ps.append(_colpiece(w_in[l], j))
        for hf in range(2):
            ps.append(_colpiece(w_pa[l], hf))
            ps.append(_colpiece(w_pb[l], hf))
            ps.append(_colpiece(w_in[l], 8 + hf))
            ps.append(_colpiece(w_in[l], 10 + hf))
        ps.append(_colpiece(w_out[l], 0))
        ps.append(_colpiece(w_out[l], 1))
        for hh in range(2):
            for j in range(4):
                ps.append(_colpiece(w_ff1[l], hh * 4 + j))
            for q in range(4):
                blk = w_ff2[l][hh * 2048:(hh + 1) * 2048, q * 256:(q + 1) * 256]
                ps.append(blk.reshape(16, 128, 256).transpose(1, 0, 2).reshape(128, 4096))
        assert len(ps) == PIECES_PER_LAYER
        for i, p in enumerate(ps):
            wst[l, i] = p
        for j in range(ADA_PIECES):
            wada[l, j] = _colpiece(w_ada[l], j)
    params = np.zeros((128, NPARAM), f)

    def put(name, arr):
        params[:, OFF[name]:OFF[name] + arr.shape[1]] = arr
    put("n1g", _vec(np.asarray(inp["norm1_g"], f)))
    put("n2g", _vec(np.asarray(inp["norm2_g"], f)))
    put("alg", _vec(np.asarray(inp["a_ln_g"], f)))
    put("alb", _vec(np.asarray(inp["a_ln_b"], f)))
    put("cvb", _vec(np.asarray(inp["b_conv_b"], f)))
    put("blg", _vec(np.asarray(inp["b_ln_g"], f)))
    put("blb", _vec(np.asarray(inp["b_ln_b"], f)))
    put("fg", np.asarray(inp["final_g"], f).reshape(8, 128).T)
    put("bada", np.asarray(inp["b_ada"], f).reshape(L, 48, 128).transpose(2, 0, 1).reshape(128, L * 48))
    put("cvw", np.asarray(inp["b_conv_w"], f).reshape(L, CW, 8, 128).transpose(3, 0, 2, 1).reshape(128, L * 8 * CW))
    a_ws = np.asarray(inp["a_ws"], f)
    wsT = np.ascontiguousarray(a_ws.transpose(3, 0, 1, 2).reshape(128, L * 8 * 128))
    mask = np.ascontiguousarray(np.tril(np.ones((128, 128), f)).T)
    bsrow = np.ascontiguousarray(np.asarray(inp["a_bs"], f).reshape(1, L * 8 * 128))
    return {
        "params": params,
        "wada": wada.reshape(L * ADA_PIECES * 128, 4096),
        "wst": wst.reshape(L * PIECES_PER_LAYER * 128, 4096),
        "wsT": wsT, "mask": mask, "bsrow": bsrow, "ident": np.eye(128, dtype=f),
    }


_NC_CACHE = {}


def kernel(**inputs):
    x = np.asarray(inputs["x"], np.float32)
    c = np.asarray(inputs["c"], np.float32)
    B, T, Dm = x.shape
    assert (B, T, Dm) == (16, SEQ, D)
    shared = prep_shared(inputs)
    per = B // NCORES
    in_maps = []
    for i in range(NCORES):
        xs = x[i * per:(i + 1) * per].reshape(per * T, D)
        m = dict(shared)
        m["xT"] = np.ascontiguousarray(xs.T)
        m["cT"] = np.ascontiguousarray(c[i * per:(i + 1) * per].reshape(per, 8, 128).transpose(2, 1, 0).reshape(128, 16))
        in_maps.append(m)
    if "nc" not in _NC_CACHE:
        _NC_CACHE["nc"] = build_nc()
    nc = _NC_CACHE["nc"]
    res = run_bass_kernel_spmd(nc, in_maps, core_ids=list(range(NCORES)))
    out = np.empty((B, T, D), np.float32)
    for i in range(NCORES):
        o = np.asarray(res.results[i]["outT"])
        out[i * per:(i + 1) * per] = o.T.reshape(per, T, D)
    return out
```
